# Optimizing a Trainium2 kernel written in Bass

```python
import math
import jax, jax.numpy as jnp
from jax import lax
import numpy as np

D_MODEL = 1024
BATCH = 8
SEQ = 2048
DEPTH = 1
DEC_BATCH = 8
DEC_SEQ = 8192
PAST_LEN = 128

HEAD_DIM = 64
N_HEADS_A = 8
N_KV_A = 2
N_HEADS_B = 8
WIN = 128
BLK = 128
GRID_W = 64
NA_ROWS_MAX = 8
NA_COLS = 16
NA_QCOLS = 16
NA_KCOLS = NA_QCOLS + NA_COLS
D_FF = 2816
ROPE_THETA = 10000.0
EPS = 1e-6
D_A = N_HEADS_A * HEAD_DIM
D_KV_A = N_KV_A * HEAD_DIM
D_B = N_HEADS_B * HEAD_DIM
D_MIX = D_A + D_B
D_IN = D_A + 2 * D_KV_A + 3 * D_B
NEG = -1e30

kernel_name = "hymba_window_gqa_neighbourhood_macaron_encoder"


def rmsnorm(x, g):
    xf = x.astype(jnp.float32)
    y = xf * lax.rsqrt(jnp.mean(xf * xf, axis=-1, keepdims=True) + EPS)
    return (y * g.astype(jnp.float32)).astype(x.dtype)


def swiglu(x, w_gu, w_down):
    g, u = jnp.split(x @ w_gu, 2, axis=-1)
    return (jax.nn.silu(g) * u) @ w_down


def rope(x):
    S = x.shape[1]
    half = HEAD_DIM // 2
    inv = ROPE_THETA ** (-jnp.arange(0, HEAD_DIM, 2, dtype=jnp.float32) / HEAD_DIM)
    ang = jnp.arange(S, dtype=jnp.float32)[:, None] * inv[None, :]
    cos = jnp.cos(ang)[None, :, None, :]
    sin = jnp.sin(ang)[None, :, None, :]
    xf = x.astype(jnp.float32)
    x1, x2 = xf[..., :half], xf[..., half:]
    return jnp.concatenate([x1 * cos - x2 * sin, x2 * cos + x1 * sin], axis=-1).astype(x.dtype)


def window_gqa(q, k, v, sink):
    B, S = q.shape[0], q.shape[1]
    nblk = S // BLK
    G = N_HEADS_A // N_KV_A
    scale = HEAD_DIM ** -0.5
    qb = q.reshape(B, nblk, BLK, N_KV_A, G, HEAD_DIM)
    kp = jnp.pad(k, ((0, 0), (WIN, WIN), (0, 0), (0, 0)))
    vp = jnp.pad(v, ((0, 0), (WIN, WIN), (0, 0), (0, 0)))
    nk = BLK + 2 * WIN
    sink_l = jnp.broadcast_to(sink.astype(jnp.float32).reshape(1, N_KV_A, G, 1, 1), (B, N_KV_A, G, BLK, 1))

    def one_block(bi):
        qi = lax.dynamic_index_in_dim(qb, bi, axis=1, keepdims=False)
        ki = lax.dynamic_slice_in_dim(kp, bi * BLK, nk, axis=1)
        vi = lax.dynamic_slice_in_dim(vp, bi * BLK, nk, axis=1)
        s = jnp.einsum('bqhgd,bkhd->bhgqk', qi, ki).astype(jnp.float32) * scale
        qpos = bi * BLK + jnp.arange(BLK)
        kpos = bi * BLK - WIN + jnp.arange(nk)
        valid = (jnp.abs(kpos[None, :] - qpos[:, None]) <= WIN) & (kpos >= 0)[None, :] & (kpos < S)[None, :]
        s = jnp.where(valid, s, NEG)
        p = jax.nn.softmax(jnp.concatenate([s, sink_l], axis=-1), axis=-1)[..., :nk]
        return jnp.einsum('bhgqk,bkhd->bqhgd', p.astype(vi.dtype), vi)

    out = lax.map(one_block, jnp.arange(nblk))
    return out.transpose(1, 0, 2, 3, 4, 5).reshape(B, S, D_A)


def neighbourhood_attn(q, k, v, rpb):
    B, S, H, hd = q.shape
    rows = S // GRID_W
    kr = min(NA_ROWS_MAX, rows)
    ncb = GRID_W // NA_QCOLS
    scale = HEAD_DIM ** -0.5
    qg = q.reshape(B, rows, ncb, NA_QCOLS, H, hd)
    kg = k.reshape(B, rows, GRID_W, H, hd)
    vg = v.reshape(B, rows, GRID_W, H, hd)
    qcol = np.arange(GRID_W).reshape(ncb, NA_QCOLS)
    c0 = np.clip(qcol - NA_COLS // 2, 0, GRID_W - NA_COLS)
    kc0 = np.clip(np.arange(ncb) * NA_QCOLS - NA_COLS // 2, 0, GRID_W - NA_KCOLS)
    kcol = kc0[:, None] + np.arange(NA_KCOLS)
    col_valid = jnp.asarray((kcol[:, None, :] >= c0[:, :, None]) & (kcol[:, None, :] < c0[:, :, None] + NA_COLS))
    col_idx = jnp.asarray(np.clip(kcol[:, None, :] - qcol[:, :, None], -(NA_COLS - 1), NA_COLS - 1) + NA_COLS - 1)
    kcol_j = jnp.asarray(kcol)

    def one_row(r):
        r0 = jnp.clip(r - kr // 2, 0, rows - kr)
        kb = lax.dynamic_slice_in_dim(kg, r0, kr, axis=1)[:, :, kcol_j]
        vb = lax.dynamic_slice_in_dim(vg, r0, kr, axis=1)[:, :, kcol_j]
        qr = lax.dynamic_index_in_dim(qg, r, axis=1, keepdims=False)
        s = jnp.einsum('bnqhd,bmnkhd->bhnqmk', qr, kb).astype(jnp.float32) * scale
        row_idx = r0 + jnp.arange(kr) - r + NA_ROWS_MAX - 1
        bias = rpb[:, row_idx[None, None, :, None], col_idx[:, :, None, :]]
        s = jnp.where(col_valid[None, None, :, :, None, :], s + bias[None].astype(jnp.float32), NEG)
        p = jax.nn.softmax(s.reshape(B, H, ncb, NA_QCOLS, kr * NA_KCOLS), axis=-1)
        p = p.reshape(B, H, ncb, NA_QCOLS, kr, NA_KCOLS).astype(vb.dtype)
        return jnp.einsum('bhnqmk,bmnkhd->bnqhd', p, vb)

    out = lax.map(one_row, jnp.arange(rows))
    return out.transpose(1, 0, 2, 3, 4, 5).reshape(B, S, D_B)


def encoder_layer(x, ffn1_pre, ffn1_w_gu, ffn1_w_down, ffn1_post, mix_pre, w_in, sink_a, rpb_b,
                  out_norm_a, out_norm_b, w_out, mix_post, ffn2_pre, ffn2_w_gu, ffn2_w_down, ffn2_post, final_norm):
    B, S, _ = x.shape
    x = x + 0.5 * rmsnorm(swiglu(rmsnorm(x, ffn1_pre), ffn1_w_gu, ffn1_w_down), ffn1_post)
    h = rmsnorm(x, mix_pre)
    proj = h @ w_in
    qa, ka, va, qb, kb, vb = jnp.split(proj, np.cumsum([D_A, D_KV_A, D_KV_A, D_B, D_B]), axis=-1)
    qa = rope(qa.reshape(B, S, N_HEADS_A, HEAD_DIM))
    ka = rope(ka.reshape(B, S, N_KV_A, HEAD_DIM))
    va = va.reshape(B, S, N_KV_A, HEAD_DIM)
    oa = window_gqa(qa, ka, va, sink_a)
    ob = neighbourhood_attn(qb.reshape(B, S, N_HEADS_B, HEAD_DIM), kb.reshape(B, S, N_HEADS_B, HEAD_DIM),
                            vb.reshape(B, S, N_HEADS_B, HEAD_DIM), rpb_b)
    mix = jnp.concatenate([rmsnorm(oa, out_norm_a), rmsnorm(ob, out_norm_b)], axis=-1) @ w_out
    x = x + rmsnorm(mix, mix_post)
    x = x + 0.5 * rmsnorm(swiglu(rmsnorm(x, ffn2_pre), ffn2_w_gu, ffn2_w_down), ffn2_post)
    return rmsnorm(x, final_norm)


def setup_inputs(seed: int = 0) -> dict:
    key = jax.random.key(seed)
    ks = jax.random.split(key, 20)
    f32 = jnp.float32

    def nrm(k, shape, scale):
        return jax.random.normal(k, shape, f32) * scale

    def gain(k, n):
        return 1.0 + 0.02 * jax.random.normal(k, (DEPTH, n), f32)

    return {
        "x_prompt": jax.random.normal(ks[0], (BATCH, SEQ, D_MODEL), f32),
        "x_sample": jax.random.normal(ks[1], (DEC_BATCH, DEC_SEQ, D_MODEL), f32),
        "ffn1_pre": gain(ks[2], D_MODEL),
        "ffn1_w_gu": nrm(ks[3], (DEPTH, D_MODEL, 2 * D_FF), D_MODEL ** -0.5),
        "ffn1_w_down": nrm(ks[4], (DEPTH, D_FF, D_MODEL), D_FF ** -0.5),
        "ffn1_post": gain(ks[5], D_MODEL),
        "mix_pre": gain(ks[6], D_MODEL),
        "w_in": nrm(ks[7], (DEPTH, D_MODEL, D_IN), D_MODEL ** -0.5),
        "sink_a": nrm(ks[8], (DEPTH, N_HEADS_A), 0.5),
        "rpb_b": nrm(ks[9], (DEPTH, N_HEADS_B, 2 * NA_ROWS_MAX - 1, 2 * NA_COLS - 1), 0.1),
        "out_norm_a": gain(ks[10], D_A),
        "out_norm_b": gain(ks[11], D_B),
        "w_out": nrm(ks[12], (DEPTH, D_MIX, D_MODEL), D_MIX ** -0.5),
        "mix_post": gain(ks[13], D_MODEL),
        "ffn2_pre": gain(ks[14], D_MODEL),
        "ffn2_w_gu": nrm(ks[15], (DEPTH, D_MODEL, 2 * D_FF), D_MODEL ** -0.5),
        "ffn2_w_down": nrm(ks[16], (DEPTH, D_FF, D_MODEL), D_FF ** -0.5),
        "ffn2_post": gain(ks[17], D_MODEL),
        "final_norm": gain(ks[18], D_MODEL),
    }


def reference(x_prompt, x_sample, ffn1_pre, ffn1_w_gu, ffn1_w_down, ffn1_post, mix_pre, w_in, sink_a, rpb_b,
              out_norm_a, out_norm_b, w_out, mix_post, ffn2_pre, ffn2_w_gu, ffn2_w_down, ffn2_post, final_norm):
    y_prompt = x_prompt
    y_sample = x_sample
    for l in range(DEPTH):
        p = (ffn1_pre[l], ffn1_w_gu[l], ffn1_w_down[l], ffn1_post[l], mix_pre[l], w_in[l], sink_a[l], rpb_b[l],
             out_norm_a[l], out_norm_b[l], w_out[l], mix_post[l], ffn2_pre[l], ffn2_w_gu[l], ffn2_w_down[l],
             ffn2_post[l], final_norm[l])
        y_prompt = encoder_layer(y_prompt, *p)
        y_sample = encoder_layer(y_sample, *p)
    return (y_prompt, y_sample)
```

```python
import contextlib
import numpy as np
import concourse.bass as bass
import concourse.mybir as mybir
from concourse.bass_utils import run_bass_kernel_spmd

F32 = mybir.dt.float32
BF16 = mybir.dt.bfloat16
AF = mybir.ActivationFunctionType
ALU = mybir.AluOpType
AX = mybir.AxisListType

D = 1024
DFF = 2816
NJ = DFF // 128
SEQ_P = 2048
SEQ_S = 8192
NTOK = SEQ_P + SEQ_S
EPS = 1e-6
NEGM = -30000.0
WIN_COLS = 2944
RING = 1024
T2 = 256
SW = 1408


class Buf:
    __slots__ = ("writer", "readers")

    def __init__(self):
        self.writer = None
        self.readers = []


class Op:
    __slots__ = ("eng", "idx", "fn", "waits", "needs_inc", "is_dma", "dsem", "dval", "incval")

    def __init__(self, eng, idx, fn, is_dma=False):
        self.eng = eng
        self.idx = idx
        self.fn = fn
        self.waits = []
        self.needs_inc = False
        self.is_dma = is_dma
        self.dsem = None
        self.dval = 0
        self.incval = 0


class Prog:
    ENGS = ("pe", "act", "dve", "pool", "sp")

    def __init__(self, nc, n_dma_sems=16):
        self.nc = nc
        self.streams = {e: [] for e in self.ENGS}
        self.seen = {e: {} for e in self.ENGS}
        self.n_dma_sems = n_dma_sems
        self.dma_rr = 0
        self.dma_last = [None] * n_dma_sems
        self.dma_cnt = [0] * n_dma_sems
        self.bufs = {}
        self.last_real = {e: None for e in self.ENGS}

    def buf(self, name):
        b = self.bufs.get(name)
        if b is None:
            b = Buf()
            self.bufs[name] = b
        return b

    def _dep(self, op, prod):
        if prod is None:
            return
        e = op.eng
        if prod.is_dma:
            key = ("d", prod.dsem)
            val = prod.dval
        else:
            key = prod.eng
            val = prod.idx
        if self.seen[e].get(key, -1) >= val:
            return
        self.seen[e][key] = val
        prod.needs_inc = True
        op.waits.append(prod)

    def op(self, eng, fn, reads=(), writes=(), is_dma=False):
        st = self.streams[eng]
        o = Op(eng, len(st), fn, is_dma)
        if is_dma:
            s = self.dma_rr
            self.dma_rr = (s + 1) % self.n_dma_sems
            prev = self.dma_last[s]
            o.dsem = s
            self.dma_cnt[s] += 1
            o.dval = self.dma_cnt[s]
            if prev is not None:
                self._dep(o, prev)
            self.dma_last[s] = o
            o.needs_inc = True
        rb = [self.buf(b) for b in reads]
        wb = [self.buf(b) for b in writes]
        for b in rb:
            self._dep(o, b.writer)
        for b in wb:
            self._dep(o, b.writer)
            for r in b.readers:
                self._dep(o, r)
        for b in rb:
            b.readers.append(o)
        for b in wb:
            b.writer = o
            b.readers = []
        st.append(o)
        if fn is not None and not is_dma:
            self.last_real[eng] = o
        return o

    def pe(self, fn, reads=(), writes=()):
        return self.op("pe", fn, reads, writes)

    def act(self, fn, reads=(), writes=()):
        return self.op("act", fn, reads, writes)

    def dve(self, fn, reads=(), writes=()):
        return self.op("dve", fn, reads, writes)

    def pool(self, fn, reads=(), writes=()):
        return self.op("pool", fn, reads, writes)

    def dma(self, fn, reads=(), writes=()):
        return self.op("sp", fn, reads, writes, is_dma=True)

    def barrier(self):
        prods = [self.last_real[e] for e in ("pe", "act", "dve", "pool")]
        prods += list(self.dma_last)
        for e in self.ENGS:
            o = Op(e, len(self.streams[e]), None)
            for p in prods:
                if p is not None:
                    self._dep(o, p)
            self.streams[e].append(o)
        self.bufs = {}

    def emit(self):
        nc = self.nc
        self.barrier()
        for e in self.ENGS:
            c = 0
            for o in self.streams[e]:
                if o.is_dma:
                    o.incval = 16 * o.dval
                elif o.needs_inc:
                    c += 1
                    o.incval = c
        with contextlib.ExitStack() as es:
            esem = {e: es.enter_context(nc.semaphore("s_" + e)) for e in self.ENGS}
            dsem = [es.enter_context(nc.semaphore("d%d" % i)) for i in range(self.n_dma_sems)]
            block = es.enter_context(nc.Block())

            def run(e, eng):
                for o in self.streams[e]:
                    for p in o.waits:
                        if p.is_dma:
                            eng.wait_ge(dsem[p.dsem], p.incval)
                        else:
                            eng.wait_ge(esem[p.eng], p.incval)
                    if o.fn is None:
                        continue
                    ins = o.fn(eng)
                    if o.is_dma:
                        ins.then_inc(dsem[o.dsem], 16)
                    elif o.needs_inc:
                        ins.then_inc(esem[e], 1)

            @block.tensor
            def _(eng):
                run("pe", eng)

            @block.scalar
            def _(eng):
                run("act", eng)

            @block.vector
            def _(eng):
                run("dve", eng)

            @block.gpsimd
            def _(eng):
                run("pool", eng)

            @block.sync
            def _(eng):
                run("sp", eng)


class Arena:
    def __init__(self, t_f32, nwords):
        self.f = t_f32
        self.b = t_f32.bitcast(BF16)
        self.off = 0
        self.cap = nwords * 4

    def alloc(self, n, dt):
        self.off = (self.off + 63) // 64 * 64
        if dt == F32:
            ap = self.f[:, self.off // 4: self.off // 4 + n]
            self.off += 4 * n
        else:
            ap = self.b[:, self.off // 2: self.off // 2 + n]
            self.off += 2 * n
        assert self.off <= self.cap, ("SBUF arena overflow", self.off, self.cap)
        return ap


class Rot:
    def __init__(self, items):
        self.items = items
        self.i = 0

    def next(self):
        it = self.items[self.i % len(self.items)]
        self.i += 1
        return it


def load_weight(P, stg, wdram, KC, N, dst, gain, eng_rr):
    for k in range(KC):
        for n0 in range(0, N, SW):
            n1 = min(N, n0 + SW)
            sap, sname = stg.next()
            P.dma(lambda e, sap=sap, k=k, n0=n0, n1=n1: e.dma_start(out=sap[:, 0:n1 - n0], in_=wdram[k * 128:(k + 1) * 128, n0:n1]),
                  writes=[sname])
            which = eng_rr[0] % 2
            eng_rr[0] += 1
            o = dst[:, k, n0:n1]
            i = sap[:, 0:n1 - n0]
            if gain is not None:
                gk = gain[:, k:k + 1]
                if which == 0:
                    P.dve(lambda e, o=o, i=i, gk=gk: e.tensor_scalar(out=o, in0=i, scalar1=gk, scalar2=None, op0=ALU.mult), reads=[sname])
                elif which == 1:
                    P.act(lambda e, o=o, i=i, gk=gk: e.activation(out=o, in_=i, func=AF.Copy, scale=gk), reads=[sname])
                else:
                    P.pool(lambda e, o=o, i=i, gk=gk: e.tensor_scalar(out=o, in0=i, scalar1=gk, scalar2=None, op0=ALU.mult), reads=[sname])
            else:
                if which == 0:
                    P.dve(lambda e, o=o, i=i: e.tensor_copy(out=o, in_=i), reads=[sname])
                elif which == 1:
                    P.act(lambda e, o=o, i=i: e.copy(out=o, in_=i), reads=[sname])
                else:
                    P.pool(lambda e, o=o, i=i: e.tensor_copy(out=o, in_=i), reads=[sname])


def build_program(debug=False, phases=(1, 2, 3)):
    nc = bass.Bass("TRN2", target_bir_lowering=False)

    def din(name, shape):
        return nc.dram_tensor(name, list(shape), F32, kind="ExternalInput").ap()

    x_d = din("x", [NTOK, D])
    w1gu_d = din("w1gu", [D, 2 * DFF])
    w1d_d = din("w1d", [DFF, D])
    w2gu_d = din("w2gu", [D, 2 * DFF])
    w2d_d = din("w2d", [DFF, D])
    win_d = din("win", [D, WIN_COLS])
    wout_d = din("wout", [D, D])
    gpre_d = din("gpre", [128, 24])
    gpost_d = din("gpost", [4, 128, D])
    gnab_d = din("gnab", [128, D])
    sink_d = din("sink", [128, 8])
    tb_d = din("tb", [128, 8 * 16 * 64])
    tbi_d = din("tbi", [128, 8 * 9 * 64])
    maska_d = din("maska", [128, 384])
    rope_d = din("rope", [4, 128, SEQ_S])
    y_d = nc.dram_tensor("y", [NTOK, D], F32, kind="ExternalOutput").ap()
    skind = "ExternalOutput" if debug else "Internal"
    xs1_d = nc.dram_tensor("xs1", [NTOK, D], F32, kind=skind).ap()
    xs2_d = nc.dram_tensor("xs2", [NTOK, D], F32, kind=skind).ap()

    with contextlib.ExitStack() as es:
        NW = 53200
        arena_t = es.enter_context(nc.sbuf_tensor("arena", [128, NW], F32))
        ps_t = es.enter_context(nc.psum_tensor("ps", [128, 4096], F32))
        psf = ps_t[:]
        psb = ps_t[:].bitcast(BF16)
        P = Prog(nc)
        eng_rr = [0]

        def bankf(i, n=512):
            return psf[:, i * 512:i * 512 + n]

        def bankb(i, n=1024):
            return psb[:, i * 1024:i * 1024 + n]

        def make_ident(A):
            identf = A.alloc(128, F32)
            ident = A.alloc(128, BF16)
            P.pool(lambda e: e.memset(identf, 0.0), writes=["identf"])
            P.pool(lambda e: e.affine_select(out=identf, in_=identf, pattern=[[-1, 128]], compare_op=ALU.not_equal,
                                             fill=1.0, base=0, channel_multiplier=1), reads=["identf"], writes=["identf"])
            P.dve(lambda e: e.tensor_copy(out=ident, in_=identf), reads=["identf"], writes=["ident"])
            return identf, ident

        def ffn_phase(src_d, dst_d, wgu_d, wd_d, gidx, pidx, final):
            A = Arena(arena_t[:], NW)
            wgu = A.alloc(8 * 2 * DFF, BF16).rearrange("p (k n) -> p k n", k=8)
            wd = A.alloc(NJ * D, BF16).rearrange("p (k n) -> p k n", k=NJ)
            gpre = A.alloc(8, F32)
            gpost = A.alloc(D, F32)
            gfin = A.alloc(D, F32) if final else None
            epst = A.alloc(1, F32)
            identf, ident = make_ident(A)
            P.pool(lambda e: e.memset(epst, EPS), writes=["epst"])
            P.dma(lambda e: e.dma_start(out=gpre, in_=gpre_d[:, gidx * 8:(gidx + 1) * 8]), writes=["gpre"])
            P.dma(lambda e: e.dma_start(out=gpost, in_=gpost_d[pidx]), writes=["gpost"])
            P.dve(lambda e: e.tensor_scalar(out=gpost, in0=gpost, scalar1=0.5, scalar2=None, op0=ALU.mult), reads=["gpost"], writes=["gpost"])
            if final:
                P.dma(lambda e: e.dma_start(out=gfin, in_=gpost_d[3]), writes=["gfin"])
            mark = A.off
            stg = Rot([(A.alloc(SW, F32), "stg%d" % i) for i in range(3)])
            P.barrier()
            load_weight(P, stg, wgu_d, 8, 2 * DFF, wgu, gpre, eng_rr)
            load_weight(P, stg, wd_d, NJ, D, wd, None, eng_rr)
            P.barrier()
            A.off = mark
            xt = Rot([(A.alloc(D, F32), "xt%d" % i) for i in range(2)])
            xn = [(A.alloc(D, BF16), "xn%d" % i) for i in range(4)]
            xnT = A.alloc(8 * 512, BF16).rearrange("p (k n) -> p k n", k=8)
            h2T = A.alloc(NJ * 512, BF16).rearrange("p (k n) -> p k n", k=NJ)
            sg = Rot([(A.alloc(512, BF16), "sg%d" % i) for i in range(2)])
            xres = Rot([(A.alloc(D, F32), "xres%d" % i) for i in range(2)])
            tt = Rot([(A.alloc(D, F32), "tt%d" % i) for i in range(2)])
            ss4 = Rot([(A.alloc(4, F32), "ss4_%d" % i) for i in range(2)])
            rs4 = Rot([(A.alloc(4, F32), "rs4_%d" % i) for i in range(2)])
            sp1 = Rot([(A.alloc(2, F32), "sp1_%d" % i) for i in range(3)])
            sf1 = Rot([(A.alloc(2, F32), "sf1_%d" % i) for i in range(3)])
            gub = Rot([(i, "pb%d" % i) for i in range(4)])
            dbank = Rot([(psf[:, 2048:3072], "pd0"), (psf[:, 3072:4096], "pd1")])
            ntile = NTOK // 512

            cur = {}

            def prep_load(t, pair):
                if pair == 0:
                    cur["s4"] = ss4.next()
                    cur["r4"] = rs4.next()
                    cur["subs"] = {}
                s4, s4n = cur["s4"]
                for s in (2 * pair, 2 * pair + 1):
                    xa, xname = xt.next()
                    xo, xon = xn[s]
                    r0 = t * 512 + s * 128
                    P.dma(lambda e, xa=xa, r0=r0: e.dma_start(out=xa, in_=src_d[r0:r0 + 128, :]), writes=[xname])
                    P.dve(lambda e, xa=xa, s=s, s4=s4, xo=xo: e.scalar_tensor_tensor(out=xo, in0=xa, scalar=1.0, in1=xa, op0=ALU.mult, op1=ALU.mult,
                                                                                       accum_out=s4[:, s:s + 1]),
                          reads=[xname], writes=[xon, s4n + "_%d" % s])
                    cur["subs"][s] = (xa, xname)

            def prep_sqrt(t, pair):
                s4, s4n = cur["s4"]
                r4, r4n = cur["r4"]
                lo = 2 * pair
                P.act(lambda e, s4=s4, lo=lo: e.activation(out=s4[:, lo:lo + 2], in_=s4[:, lo:lo + 2], func=AF.Sqrt, scale=1.0 / D, bias=epst),
                      reads=[s4n + "_%d" % lo, s4n + "_%d" % (lo + 1), "epst"], writes=[s4n + "p%d" % lo])
                P.dve(lambda e, s4=s4, r4=r4, lo=lo: e.reciprocal(out=r4[:, lo:lo + 2], in_=s4[:, lo:lo + 2]),
                      reads=[s4n + "p%d" % lo], writes=[r4n + "p%d" % lo])

            def prep_scale(t, pair):
                r4, r4n = cur["r4"]
                lo = 2 * pair
                for q in (lo, lo + 1):
                    xa2, xname2 = cur["subs"][q]
                    xo, xon = xn[q]
                    P.act(lambda e, xo=xo, xa2=xa2, r4=r4, q=q: e.activation(out=xo, in_=xa2, func=AF.Copy, scale=r4[:, q:q + 1]),
                          reads=[xname2, r4n + "p%d" % lo], writes=[xon])

            def prep_T(s):
                xo, xon = xn[s]
                bi, bn = gub.next()

                def tr(e):
                    for k in range(8):
                        i = e.transpose(out=bankb(bi)[:, k * 128:(k + 1) * 128], in_=xo[:, k * 128:(k + 1) * 128], identity=ident)
                    return i
                P.pe(tr, reads=[xon, "ident"], writes=[bn])
                P.dve(lambda e, s=s: e.tensor_copy(out=xnT[:, :, s * 128:(s + 1) * 128], in_=bankb(bi).rearrange("p (k n) -> p k n", k=8)),
                      reads=[bn], writes=["xnT"])

            def gu(j):
                gi, gbn = gub.next()
                gb = bankf(gi)
                sga, sgn = sg.next()

                def mmg(e):
                    for k in range(8):
                        i = e.matmul(gb, lhsT=wgu[:, k, j * 128:(j + 1) * 128], rhs=xnT[:, k, :], start=(k == 0), stop=(k == 7))
                    return i
                P.pe(mmg, reads=["xnT"], writes=[gbn])
                P.act(lambda e: e.activation(out=sga, in_=gb, func=AF.Silu), reads=[gbn], writes=[sgn])
                ui, ubn = gub.next()
                ub = bankf(ui)

                def mmu(e):
                    for k in range(8):
                        i = e.matmul(ub, lhsT=wgu[:, k, DFF + j * 128:DFF + (j + 1) * 128], rhs=xnT[:, k, :], start=(k == 0), stop=(k == 7))
                    return i
                P.pe(mmu, reads=["xnT"], writes=[ubn])
                P.dve(lambda e: e.tensor_tensor(out=h2T[:, j, :], in0=ub, in1=sga, op=ALU.mult), reads=[ubn, sgn], writes=["h2T"])

            def down(t, s):
                db, dbn = dbank.next()

                def mm(e):
                    for n in range(2):
                        for j in range(NJ):
                            i = e.matmul(db[:, n * 512:(n + 1) * 512], lhsT=h2T[:, j, s * 128:(s + 1) * 128], rhs=wd[:, j, n * 512:(n + 1) * 512],
                                         start=(j == 0), stop=(j == NJ - 1))
                    return i
                P.pe(mm, reads=["h2T"], writes=[dbn])
                r0 = t * 512 + s * 128
                xr, xrn = xres.next()
                ta, tan = tt.next()
                s1, s1n = sp1.next()
                f1, f1n = sf1.next()
                P.dma(lambda e: e.dma_start(out=xr, in_=src_d[r0:r0 + 128, :]), writes=[xrn])
                P.act(lambda e: e.activation(out=ta, in_=db, func=AF.Square, accum_out=s1[:, 0:1]), reads=[dbn], writes=[tan, s1n])
                P.act(lambda e: e.activation(out=s1[:, 0:1], in_=s1[:, 0:1], func=AF.Sqrt, scale=1.0 / D, bias=epst), reads=[s1n, "epst"], writes=[s1n])
                P.dve(lambda e: e.reciprocal(out=s1[:, 1:2], in_=s1[:, 0:1]), reads=[s1n], writes=[s1n])
                P.dve(lambda e: e.scalar_tensor_tensor(out=ta, in0=db, scalar=s1[:, 1:2], in1=gpost, op0=ALU.mult, op1=ALU.mult),
                      reads=[dbn, s1n, "gpost"], writes=[tan])
                if not final:
                    P.pool(lambda e: e.tensor_tensor(out=xr, in0=xr, in1=ta, op=ALU.add), reads=[xrn, tan], writes=[xrn])
                    P.dma(lambda e: e.dma_start(out=dst_d[r0:r0 + 128, :], in_=xr), reads=[xrn], writes=["out%d" % r0])
                else:
                    P.pool(lambda e: e.tensor_tensor(out=xr, in0=xr, in1=ta, op=ALU.add), reads=[xrn, tan], writes=[xrn])
                    P.dve(lambda e: e.scalar_tensor_tensor(out=ta, in0=xr, scalar=1.0, in1=xr, op0=ALU.mult, op1=ALU.mult, accum_out=f1[:, 0:1]),
                          reads=[xrn], writes=[tan, f1n])
                    P.act(lambda e: e.activation(out=f1[:, 0:1], in_=f1[:, 0:1], func=AF.Sqrt, scale=1.0 / D, bias=epst), reads=[f1n, "epst"], writes=[f1n])
                    P.dve(lambda e: e.reciprocal(out=f1[:, 1:2], in_=f1[:, 0:1]), reads=[f1n], writes=[f1n])
                    P.dve(lambda e: e.scalar_tensor_tensor(out=ta, in0=xr, scalar=f1[:, 1:2], in1=gfin, op0=ALU.mult, op1=ALU.mult),
                          reads=[xrn, f1n, "gfin"], writes=[tan])
                    P.dma(lambda e: e.dma_start(out=dst_d[r0:r0 + 128, :], in_=ta), reads=[tan], writes=["out%d" % r0])

            for pair in range(2):
                prep_load(0, pair)
                prep_sqrt(0, pair)
                prep_scale(0, pair)
            for s in range(4):
                prep_T(s)
            sched = {1: (prep_load, 0), 4: (prep_sqrt, 0), 5: (prep_scale, 0), 8: (prep_load, 1), 11: (prep_sqrt, 1), 12: (prep_scale, 1)}
            for t in range(ntile):
                for j in range(NJ):
                    gu(j)
                    if j in sched and t + 1 < ntile:
                        fn, pair = sched[j]
                        fn(t + 1, pair)
                for s in range(4):
                    if t + 1 < ntile:
                        prep_T(s)
                    down(t, s)
            P.barrier()

        def mixer_phase():
            A = Arena(arena_t[:], NW)
            win = A.alloc(8 * WIN_COLS, BF16).rearrange("p (k n) -> p k n", k=8)
            wout = A.alloc(8 * D, BF16).rearrange("p (k n) -> p k n", k=8)
            tb = A.alloc(8 * 16 * 64, BF16)
            tbi = A.alloc(8 * 9 * 64, BF16)
            maska = A.alloc(384, BF16)
            negt = A.alloc(64, BF16)
            gmix = A.alloc(8, F32)
            gpost = A.alloc(D, F32)
            gnab = A.alloc(D, F32)
            sink8 = A.alloc(8, F32)
            epst = A.alloc(1, F32)
            identf, ident = make_ident(A)
            idh0 = A.alloc(128, BF16)
            idh1 = A.alloc(128, BF16)
            P.pool(lambda e: e.memset(epst, EPS), writes=["epst"])
            P.pool(lambda e: e.memset(negt, NEGM), writes=["negt"])
            P.pool(lambda e: e.memset(idh0, 0.0), writes=["idh0"])
            P.pool(lambda e: e.memset(idh1, 0.0), writes=["idh1"])
            P.dve(lambda e: e.tensor_copy(out=idh0[0:64, :], in_=identf[0:64, :]), reads=["identf", "idh0"], writes=["idh0"])
            P.dve(lambda e: e.tensor_copy(out=idh1[64:128, :], in_=identf[64:128, :]), reads=["identf", "idh1"], writes=["idh1"])
            P.dma(lambda e: e.dma_start(out=gmix, in_=gpre_d[:, 8:16]), writes=["gmix"])
            P.dma(lambda e: e.dma_start(out=gpost, in_=gpost_d[1]), writes=["gpost"])
            P.dma(lambda e: e.dma_start(out=gnab, in_=gnab_d), writes=["gnab"])
            P.dma(lambda e: e.dma_start(out=sink8, in_=sink_d), writes=["sink8"])
            mark = A.off
            stg = Rot([(A.alloc(SW, F32), "stg%d" % i) for i in range(3)])
            P.barrier()
            load_weight(P, stg, win_d, 8, WIN_COLS, win, gmix, eng_rr)
            load_weight(P, stg, wout_d, 8, D, wout, None, eng_rr)
            for n0 in range(0, 8 * 16 * 64, SW):
                n1 = min(8 * 16 * 64, n0 + SW)
                sap, sname = stg.next()
                P.dma(lambda e, sap=sap, n0=n0, n1=n1: e.dma_start(out=sap[:, 0:n1 - n0], in_=tb_d[:, n0:n1]), writes=[sname])
                P.dve(lambda e, sap=sap, n0=n0, n1=n1: e.tensor_copy(out=tb[:, n0:n1], in_=sap[:, 0:n1 - n0]), reads=[sname])
            for n0 in range(0, 8 * 9 * 64, SW):
                n1 = min(8 * 9 * 64, n0 + SW)
                sap, sname = stg.next()
                P.dma(lambda e, sap=sap, n0=n0, n1=n1: e.dma_start(out=sap[:, 0:n1 - n0], in_=tbi_d[:, n0:n1]), writes=[sname])
                P.dve(lambda e, sap=sap, n0=n0, n1=n1: e.tensor_copy(out=tbi[:, n0:n1], in_=sap[:, 0:n1 - n0]), reads=[sname])
            sap, sname = stg.next()
            P.dma(lambda e, sap=sap: e.dma_start(out=sap[:, 0:384], in_=maska_d), writes=[sname])
            P.dve(lambda e, sap=sap: e.tensor_copy(out=maska, in_=sap[:, 0:384]), reads=[sname])
            P.barrier()
            A.off = mark
            tbv = tb.rearrange("p (h j c) -> p h j c", h=8, j=16)
            tbiv = tbi.rearrange("p (h n) -> p h n", h=8)
            NBLK = RING // 128
            KaT = A.alloc(RING, BF16)
            Va = A.alloc(NBLK * 128, BF16).rearrange("p (b n) -> p b n", b=NBLK)
            KbT = A.alloc(4 * RING, BF16).rearrange("p (c n) -> p c n", c=4)
            Vb = A.alloc(NBLK * 512, BF16).rearrange("p (b n) -> p b n", b=NBLK)
            QaT = [A.alloc(4 * T2, BF16).rearrange("p (c n) -> p c n", c=4) for _ in range(3)]
            QbT = [A.alloc(4 * T2, BF16).rearrange("p (c n) -> p c n", c=4) for _ in range(3)]
            xnT = A.alloc(8 * T2, BF16).rearrange("p (k n) -> p k n", k=8)
            xt = Rot([(A.alloc(D, F32), "xt%d" % i) for i in range(2)])
            xn = [(A.alloc(D, BF16), "xn%d" % i) for i in range(2)]
            ropeb = Rot([(A.alloc(4 * T2, F32).rearrange("p (f n) -> p f n", f=4), "rope%d" % i) for i in range(2)])
            t1 = Rot([(A.alloc(T2, F32), "t1_%d" % i) for i in range(2)])
            t2 = Rot([(A.alloc(T2, F32), "t2_%d" % i) for i in range(2)])
            Pa = Rot([(A.alloc(384, BF16), "Pa%d" % i) for i in range(2)])
            PTa = Rot([(A.alloc(384, BF16), "PTa%d" % i) for i in range(2)])
            Pb = Rot([(A.alloc(576, BF16), "Pb%d" % i) for i in range(2)])
            PTb = Rot([(A.alloc(640, BF16), "PTb%d" % i) for i in range(2)])
            oab = Rot([(A.alloc(D, F32), "oab%d" % i) for i in range(2)])
            on = Rot([(A.alloc(D, BF16), "on%d" % i) for i in range(2)])
            onT = Rot([(A.alloc(D, BF16).rearrange("p (k n) -> p k n", k=8), "onT%d" % i) for i in range(2)])
            xres = Rot([(A.alloc(D, F32), "xres%d" % i) for i in range(2)])
            tt = Rot([(A.alloc(D, F32), "tt%d" % i) for i in range(2)])
            st2 = Rot([(A.alloc(4, F32), "st2_%d" % i) for i in range(2)])
            sta = Rot([(A.alloc(40, F32), "sta%d" % i) for i in range(4)])
            stb = Rot([(A.alloc(24, F32), "stb%d" % i) for i in range(4)])
            sn = Rot([(A.alloc(4, F32), "sn%d" % i) for i in range(6)])
            wb = Rot([(0, "pw0")])
            abank = Rot([(1, "pA1"), (2, "pA2")])

            def ring_pieces(tok0, n):
                out = []
                off = 0
                while off < n:
                    c = (tok0 + off) % RING
                    ln = min(n - off, RING - c)
                    out.append((c, ln, off))
                    off += ln
                return out

            def stage_a_gen(seq0, S, i, g):
                tok0 = i * T2
                sl = g % 3
                s2, s2n = st2.next()
                subs = []
                for s in range(2):
                    xa, xname = xt.next()
                    r0 = seq0 + tok0 + s * 128
                    P.dma(lambda e, xa=xa, r0=r0: e.dma_start(out=xa, in_=xs1_d[r0:r0 + 128, :]), writes=[xname])
                    P.dve(lambda e, xa=xa, s=s: e.scalar_tensor_tensor(out=xn[s][0], in0=xa, scalar=1.0, in1=xa, op0=ALU.mult, op1=ALU.mult,
                                                                        accum_out=s2[:, s:s + 1]),
                          reads=[xname], writes=[xn[s][1], s2n + "_%d" % s])
                    subs.append((xa, xname))
                rp, rpn = ropeb.next()
                P.dma(lambda e: e.dma_start(out=rp, in_=rope_d[:, :, tok0:tok0 + T2].rearrange("f p n -> p f n")), writes=[rpn])
                yield
                P.act(lambda e: e.activation(out=s2[:, 0:2], in_=s2[:, 0:2], func=AF.Ln, scale=1.0 / D, bias=epst),
                      reads=[s2n + "_0", s2n + "_1", "epst"], writes=[s2n + "p"])
                P.act(lambda e: e.activation(out=s2[:, 2:4], in_=s2[:, 0:2], func=AF.Exp, scale=-0.5), reads=[s2n + "p"], writes=[s2n + "r"])
                yield
                for s in range(2):
                    xa, xname = subs[s]
                    xo, xon = xn[s]
                    P.act(lambda e, xo=xo, xa=xa, s=s: e.activation(out=xo, in_=xa, func=AF.Copy, scale=s2[:, 2 + s:3 + s]),
                          reads=[xname, s2n + "r"], writes=[xon])
                yield
                for s in range(2):
                    xo, xon = xn[s]
                    bi, bn = wb.next()

                    def tr(e, xo=xo, bi=bi):
                        for k in range(8):
                            ins = e.transpose(out=bankb(bi)[:, k * 128:(k + 1) * 128], in_=xo[:, k * 128:(k + 1) * 128], identity=ident)
                        return ins
                    P.pe(tr, reads=[xon, "ident"], writes=[bn])
                    P.dve(lambda e, s=s, bi=bi: e.tensor_copy(out=xnT[:, :, s * 128:(s + 1) * 128], in_=bankb(bi).rearrange("p (k n) -> p k n", k=8)),
                          reads=[bn], writes=["xnT"])
                    yield
                rc = tok0 % RING
                kblk = ["kv%d" % ((tok0 // 128 + s) % NBLK) for s in range(2)]

                def wmm(e, bi, col0, half):
                    for k in range(8):
                        ins = e.matmul(bankf(bi)[:, half * T2:(half + 1) * T2], lhsT=win[:, k, col0:col0 + 128], rhs=xnT[:, k, :],
                                       start=(k == 0), stop=(k == 7))
                    return ins
                for c in range(5):
                    bi, bn = wb.next()
                    pc, sc = (c, 4 + c) if c < 4 else (8, 9)

                    def mm(e, bi=bi, pc=pc, sc=sc):
                        wmm(e, bi, pc * 128, 0)
                        return wmm(e, bi, sc * 128, 1)
                    P.pe(mm, reads=["xnT"], writes=[bn])
                    ta, tan = t1.next()
                    tb2, tbn = t2.next()
                    f0 = 0 if c < 4 else 2
                    P.dve(lambda e, bi=bi, ta=ta, f0=f0: e.tensor_tensor(out=ta, in0=bankf(bi)[:, 0:T2], in1=rp[:, f0, :], op=ALU.mult),
                          reads=[bn, rpn], writes=[tan])
                    P.dve(lambda e, bi=bi, tb2=tb2, f0=f0: e.tensor_tensor(out=tb2, in0=bankf(bi)[:, T2:2 * T2], in1=rp[:, f0 + 1, :], op=ALU.mult),
                          reads=[bn, rpn], writes=[tbn])
                    if c < 4:
                        dst = QaT[sl][:, c, :]
                        wn = ["QaT%d" % sl]
                    else:
                        dst = KaT[:, rc:rc + T2]
                        wn = ["ka" + b for b in kblk]
                    P.pool(lambda e, dst=dst, ta=ta, tb2=tb2: e.tensor_tensor(out=dst, in0=ta, in1=tb2, op=ALU.add), reads=[tan, tbn], writes=wn)
                    yield
                for c in range(4):
                    bi, bn = wb.next()

                    def mm(e, bi=bi, c=c):
                        wmm(e, bi, (10 + c) * 128, 0)
                        return wmm(e, bi, (14 + c) * 128, 1)
                    P.pe(mm, reads=["xnT"], writes=[bn])
                    P.act(lambda e, bi=bi, c=c: e.activation(out=QbT[sl][:, c, :], in_=bankf(bi)[:, 0:T2], func=AF.Copy, scale=0.125),
                          reads=[bn], writes=["QbT%d" % sl])
                    P.act(lambda e, bi=bi, c=c: e.copy(out=KbT[:, c, rc:rc + T2], in_=bankf(bi)[:, T2:2 * T2]),
                          reads=[bn], writes=["kb" + b for b in kblk])
                    yield
                for s in range(2):
                    blk = (tok0 // 128 + s) % NBLK
                    bi, bn = wb.next()

                    def mmva(e, bi=bi, s=s):
                        for k in range(8):
                            ins = e.matmul(bankf(bi)[:, 0:128], lhsT=xnT[:, k, s * 128:(s + 1) * 128], rhs=win[:, k, 2304:2432], start=(k == 0), stop=(k == 7))
                        return ins
                    P.pe(mmva, reads=["xnT"], writes=[bn])
                    P.act(lambda e, bi=bi, blk=blk: e.copy(out=Va[:, blk, :], in_=bankf(bi)[:, 0:128]), reads=[bn], writes=["va" + kblk[s]])
                    yield
                    bi2, bn2 = wb.next()

                    def mmvb(e, bi2=bi2, s=s):
                        for k in range(8):
                            ins = e.matmul(bankf(bi2), lhsT=xnT[:, k, s * 128:(s + 1) * 128], rhs=win[:, k, 2432:2944], start=(k == 0), stop=(k == 7))
                        return ins
                    P.pe(mmvb, reads=["xnT"], writes=[bn2])
                    P.dve(lambda e, bi2=bi2, blk=blk: e.tensor_copy(out=Vb[:, blk, :], in_=bankf(bi2)), reads=[bn2], writes=["vb" + kblk[s]])
                    yield

            class Unit:
                pass

            def unit_qk(kind, S, i, g, u, h, stat, statn):
                U = Unit()
                sl = g % 3
                qtok0 = i * T2 + u * 128
                U.kind, U.h, U.stat, U.statn = kind, h, stat, statn
                if kind == "a":
                    nb = S // 128
                    b = qtok0 // 128
                    kb0 = max(b - 1, 0)
                    kb1 = min(b + 1, nb - 1)
                    nk = (kb1 - kb0 + 1) * 128
                    ktok0 = kb0 * 128
                    pb = 0 if h < 4 else 64
                    lhsT = QaT[sl][pb:pb + 64, h % 4, u * 128:(u + 1) * 128]
                    qn = "QaT%d" % sl
                    abi, abn = abank.next()
                    sbank = psf[:, abi * 512:abi * 512 + nk]
                    sname = [abn]
                    mcol0 = (kb0 - (b - 1)) * 128
                    U.vcol = (h // 4) * 64
                    U.ptv = bankb(abi)[:, 0:384]
                    U.pbuf, U.pbn = Pa.next()
                    U.ptbuf, U.ptn = PTa.next()
                    U.ob = bankf(6)
                    U.obn = "pOa"
                    U.V = Va
                    U.ptname = abn
                    nr = 0
                else:
                    rows = S // 64
                    r = qtok0 // 64
                    if r < 4:
                        kr0, nr = 0, 8
                    elif r >= rows - 4:
                        kr0, nr = rows - 8, 8
                    else:
                        kr0, nr = r - 4, 9
                    nk = nr * 64
                    ktok0 = kr0 * 64
                    pb = (h % 2) * 64
                    lhsT = QbT[sl][pb:pb + 64, h // 2, u * 128:(u + 1) * 128]
                    qn = "QbT%d" % sl
                    sbank = psf[:, 3 * 512:3 * 512 + nk]
                    sname = ["pB"]
                    j0 = kr0 - r + 7
                    U.vcol = h * 64
                    U.ptv = bankb(5)[:, 0:640]
                    U.pbuf, U.pbn = Pb.next()
                    U.ptbuf, U.ptn = PTb.next()
                    U.ob = bankf(7)
                    U.obn = "pOb"
                    U.V = Vb
                    U.ptname = "pPTb"
                U.nk = nk
                U.nfull = nk // 128
                U.rem = nk - U.nfull * 128
                U.kblks = [((ktok0 // 128 + q) % NBLK) for q in range(U.nfull + (1 if U.rem else 0))]
                kpre = "ka" if kind == "a" else "kb"
                vpre = "va" if kind == "a" else "vb"
                kreads = [kpre + "kv%d" % q for q in U.kblks]
                U.vreads = [vpre + "kv%d" % q for q in U.kblks]

                def qk(e):
                    segs = [(0, min(nk, 512))] + ([(512, nk)] if nk > 512 else [])
                    for (a0, a1) in segs:
                        if kind == "a":
                            e.matmul(sbank[:, a0:a1], lhsT=ident, rhs=maska[:, mcol0 + a0:mcol0 + a1], start=True, stop=False)
                        elif nr == 9:
                            e.matmul(sbank[:, a0:a1], lhsT=ident, rhs=tbiv[:, h, a0:a1], start=True, stop=False)
                        else:
                            rhs = tbv[:, h, j0:j0 + nr, :].rearrange("p j c -> p (j c)")
                            e.matmul(sbank[:, a0:a1], lhsT=ident, rhs=rhs[:, a0:a1], start=True, stop=False)
                    pieces = []
                    for (c0, ln, off) in ring_pieces(ktok0, nk):
                        if off < 512 < off + ln:
                            pieces.append((c0, 512 - off, off))
                            pieces.append((c0 + 512 - off, off + ln - 512, 512))
                        else:
                            pieces.append((c0, ln, off))
                    ins = None
                    for pi, (c0, ln, off) in enumerate(pieces):
                        last_in_seg = (pi == len(pieces) - 1) or (pieces[pi + 1][2] >= 512 > off)
                        if kind == "a":
                            rhs = KaT[pb:pb + 64, c0:c0 + ln]
                        else:
                            rhs = KbT[pb:pb + 64, h // 2, c0:c0 + ln]
                        ins = e.matmul(sbank[:, off:off + ln], lhsT=lhsT, rhs=rhs, start=False, stop=last_in_seg)
                    return ins
                P.pe(qk, reads=[qn, "ident"] + kreads, writes=sname)
                P.dve(lambda e: e.tensor_reduce(out=stat[:, h:h + 1], in_=sbank, axis=AX.X, op=ALU.max, negate=True),
                      reads=sname, writes=[statn + "m%d" % h])
                P.act(lambda e: e.activation(out=U.pbuf[:, 0:nk], in_=sbank, func=AF.Exp, bias=stat[:, h:h + 1], scale=1.0, accum_out=stat[:, 8 + h:9 + h]),
                      reads=sname + [statn + "m%d" % h], writes=[U.pbn, statn + "s%d" % h])
                return U

            def unit_trp(U):
                nfull, rem, pbuf, ptv, ptbuf = U.nfull, U.rem, U.pbuf, U.ptv, U.ptbuf

                def trp(e):
                    for q in range(nfull):
                        ins = e.transpose(out=ptv[:, q * 128:(q + 1) * 128], in_=pbuf[:, q * 128:(q + 1) * 128], identity=ident)
                    if rem:
                        ins = e.transpose(out=ptv[0:rem, nfull * 128:(nfull + 1) * 128], in_=pbuf[:, nfull * 128:nfull * 128 + rem], identity=ident)
                    return ins
                P.pe(trp, reads=[U.pbn, "ident"], writes=[U.ptname])
                if U.kind == "a":
                    P.dve(lambda e: e.tensor_copy(out=ptbuf[:, 0:nfull * 128], in_=ptv[:, 0:nfull * 128]), reads=[U.ptname], writes=[U.ptn])
                else:
                    P.dve(lambda e: e.tensor_copy(out=ptbuf[:, 0:nfull * 128], in_=ptv[:, 0:nfull * 128]), reads=[U.ptname], writes=[U.ptn])
                if rem:
                    P.dve(lambda e: e.tensor_copy(out=ptbuf[0:rem, nfull * 128:(nfull + 1) * 128], in_=ptv[0:rem, nfull * 128:(nfull + 1) * 128]),
                          reads=[U.ptname], writes=[U.ptn + "r"])

            def unit_pv(U):
                nfull, rem, ptbuf, h = U.nfull, U.rem, U.ptbuf, U.h

                def pv(e):
                    nq = nfull + (1 if rem else 0)
                    for q in range(nq):
                        kp = 128 if q < nfull else rem
                        ins = e.matmul(U.ob[:, h * 64:(h + 1) * 64], lhsT=ptbuf[0:kp, q * 128:(q + 1) * 128], rhs=U.V[0:kp, U.kblks[q], U.vcol:U.vcol + 64],
                                       start=(q == 0), stop=(q == nq - 1))
                    return ins
                P.pe(pv, reads=[U.ptn] + ([U.ptn + "r"] if rem else []) + U.vreads, writes=[U.obn + "h%d" % h])

            def finalize_gen(seq0, i, u, sa, san, sb_, sbn):
                oa, oan = oab.next()
                onb, onn = on.next()
                ms = ["%sm%d" % (san, h) for h in range(8)]
                ssn = ["%ss%d" % (san, h) for h in range(8)]
                P.dve(lambda e: e.tensor_tensor(out=sa[:, 16:24], in0=sa[:, 0:8], in1=sink8, op=ALU.add), reads=ms + ["sink8"], writes=[san + "t"])
                P.act(lambda e: e.activation(out=sa[:, 16:24], in_=sa[:, 16:24], func=AF.Exp), reads=[san + "t"], writes=[san + "t"])
                P.dve(lambda e: e.tensor_tensor(out=sa[:, 24:32], in0=sa[:, 16:24], in1=sa[:, 8:16], op=ALU.add), reads=[san + "t"] + ssn, writes=[san + "d"])
                P.dve(lambda e: e.reciprocal(out=sa[:, 32:40], in_=sa[:, 24:32]), reads=[san + "d"], writes=[san + "r"])
                P.dve(lambda e: e.tensor_tensor(out=oa[:, 0:512].rearrange("p (h d) -> p h d", h=8),
                                                in0=bankf(6).rearrange("p (h d) -> p h d", h=8),
                                                in1=sa[:, 32:40].unsqueeze(2).broadcast_to([128, 8, 64]), op=ALU.mult),
                      reads=["pOah%d" % h for h in range(8)] + [san + "r"], writes=[oan + "a"])
                sbs = ["%ss%d" % (sbn, h) for h in range(8)]
                P.dve(lambda e: e.reciprocal(out=sb_[:, 16:24], in_=sb_[:, 8:16]), reads=sbs, writes=[sbn + "r"])
                P.dve(lambda e: e.tensor_tensor(out=oa[:, 512:1024].rearrange("p (h d) -> p h d", h=8),
                                                in0=bankf(7).rearrange("p (h d) -> p h d", h=8),
                                                in1=sb_[:, 16:24].unsqueeze(2).broadcast_to([128, 8, 64]), op=ALU.mult),
                      reads=["pObh%d" % h for h in range(8)] + [sbn + "r"], writes=[oan + "b"])
                s3, s3n = sn.next()
                P.dve(lambda e: e.scalar_tensor_tensor(out=onb[:, 0:512], in0=oa[:, 0:512], scalar=1.0, in1=oa[:, 0:512], op0=ALU.mult, op1=ALU.mult,
                                                       accum_out=s3[:, 0:1]), reads=[oan + "a"], writes=[onn + "0", s3n + "a"])
                P.dve(lambda e: e.scalar_tensor_tensor(out=onb[:, 512:1024], in0=oa[:, 512:1024], scalar=1.0, in1=oa[:, 512:1024], op0=ALU.mult, op1=ALU.mult,
                                                       accum_out=s3[:, 1:2]), reads=[oan + "b"], writes=[onn + "1", s3n + "b"])
                yield
                P.act(lambda e: e.activation(out=s3[:, 0:2], in_=s3[:, 0:2], func=AF.Ln, scale=1.0 / 512, bias=epst),
                      reads=[s3n + "a", s3n + "b", "epst"], writes=[s3n + "p"])
                P.act(lambda e: e.activation(out=s3[:, 2:4], in_=s3[:, 0:2], func=AF.Exp, scale=-0.5), reads=[s3n + "p"], writes=[s3n + "r"])
                for q in range(2):
                    P.dve(lambda e, q=q: e.scalar_tensor_tensor(out=onb[:, q * 512:(q + 1) * 512], in0=oa[:, q * 512:(q + 1) * 512],
                                                                scalar=s3[:, 2 + q:3 + q], in1=gnab[:, q * 512:(q + 1) * 512],
                                                                op0=ALU.mult, op1=ALU.mult),
                          reads=[oan + ("a" if q == 0 else "b"), s3n + "r", "gnab"], writes=[onn + "%d" % q])
                yield
                oT, oTn = onT.next()
                bi, bn = wb.next()

                def tr(e):
                    for k in range(8):
                        ins = e.transpose(out=bankb(bi)[:, k * 128:(k + 1) * 128], in_=onb[:, k * 128:(k + 1) * 128], identity=ident)
                    return ins
                P.pe(tr, reads=[onn + "0", onn + "1", "ident"], writes=[bn])
                P.dve(lambda e: e.tensor_copy(out=oT, in_=bankb(bi).rearrange("p (k n) -> p k n", k=8)), reads=[bn], writes=[oTn])
                r0 = seq0 + i * T2 + u * 128
                xr, xrn = xres.next()
                ta, tan = tt.next()
                s4, s4n = sn.next()
                P.dma(lambda e: e.dma_start(out=xr, in_=xs1_d[r0:r0 + 128, :]), writes=[xrn])
                yield

                for n in range(2):
                    def mmo(e, n=n):
                        for k in range(8):
                            ins = e.matmul(bankf(0), lhsT=oT[:, k, :], rhs=wout[:, k, n * 512:(n + 1) * 512], start=(k == 0), stop=(k == 7))
                        return ins
                    P.pe(mmo, reads=[oTn], writes=["pw0"])
                    P.act(lambda e, n=n: e.copy(out=ta[:, n * 512:(n + 1) * 512], in_=bankf(0)), reads=["pw0"], writes=[tan + "h%d" % n])
                    yield
                P.dve(lambda e: e.scalar_tensor_tensor(out=onb, in0=ta, scalar=1.0, in1=ta, op0=ALU.mult, op1=ALU.mult, accum_out=s4[:, 0:1]),
                      reads=[tan + "h0", tan + "h1"], writes=[onn + "0", onn + "1", s4n])
                P.act(lambda e: e.activation(out=s4[:, 0:1], in_=s4[:, 0:1], func=AF.Ln, scale=1.0 / D, bias=epst), reads=[s4n, "epst"], writes=[s4n])
                P.act(lambda e: e.activation(out=s4[:, 1:2], in_=s4[:, 0:1], func=AF.Exp, scale=-0.5), reads=[s4n], writes=[s4n])
                P.dve(lambda e: e.scalar_tensor_tensor(out=ta, in0=ta, scalar=s4[:, 1:2], in1=gpost, op0=ALU.mult, op1=ALU.mult),
                      reads=[tan + "h0", tan + "h1", s4n, "gpost"], writes=[tan + "h0", tan + "h1"])
                yield
                P.pool(lambda e: e.tensor_tensor(out=xr, in0=xr, in1=ta, op=ALU.add), reads=[xrn, tan + "h0", tan + "h1"], writes=[xrn])
                P.dma(lambda e: e.dma_start(out=xs2_d[r0:r0 + 128, :], in_=xr), reads=[xrn], writes=["o2_%d" % r0])

            pending = []

            def pump(filler):
                if filler is not None:
                    next(filler, None)
                for gg in list(pending):
                    try:
                        next(gg)
                    except StopIteration:
                        pending.remove(gg)

            def stage_b(seq0, S, i, g, filler):
                order = [(u, h) for u in range(2) for h in range(8)]
                stats = {}
                for u in range(2):
                    stats[u] = (sta.next(), stb.next())

                def qk_pair(u, h):
                    (sa, san), (sb_, sbn) = stats[u]
                    return (unit_qk("a", S, i, g, u, h, sa, san), unit_qk("b", S, i, g, u, h, sb_, sbn))
                cur = qk_pair(*order[0])
                for p, (u, h) in enumerate(order):
                    ua, ub = cur
                    unit_trp(ua)
                    unit_trp(ub)
                    if p + 1 < len(order):
                        cur = qk_pair(*order[p + 1])
                    pump(filler)
                    unit_pv(ua)
                    unit_pv(ub)
                    if h == 7:
                        (sa, san), (sb_, sbn) = stats[u]
                        gg = finalize_gen(seq0, i, u, sa, san, sb_, sbn)
                        next(gg)
                        pending.append(gg)

            tiles = []
            g = 0
            for (seq0, S) in ((0, SEQ_P), (SEQ_P, SEQ_S)):
                for i in range(S // T2):
                    tiles.append((seq0, S, i, g))
                    g += 1
            for g in range(2):
                for _ in stage_a_gen(*tiles[g]):
                    pass
            for g in range(len(tiles)):
                filler = stage_a_gen(*tiles[g + 2]) if g + 2 < len(tiles) else None
                seq0, S, i, _ = tiles[g]
                stage_b(seq0, S, i, g, filler)
                if filler is not None:
                    for _ in filler:
                        pass
            while pending:
                pump(None)
            P.barrier()

        if 1 in phases:
            ffn_phase(x_d, xs1_d, w1gu_d, w1d_d, 0, 0, False)
        if 2 in phases:
            mixer_phase()
        if 3 in phases:
            ffn_phase(xs2_d, y_d, w2gu_d, w2d_d, 2, 2, True)
        P.emit()
    return nc


def _win_cols():
    def hc(base, h):
        return list(base + h * 64 + np.arange(64))

    def hs(base, h):
        return list(base + h * 64 + np.concatenate([np.arange(32, 64), np.arange(0, 32)]))
    cols = []
    for c in range(4):
        cols += hc(0, c) + hc(0, 4 + c)
    for c in range(4):
        cols += hs(0, c) + hs(0, 4 + c)
    cols += hc(512, 0) + hc(512, 1)
    cols += hs(512, 0) + hs(512, 1)
    for c in range(4):
        cols += hc(768, 2 * c) + hc(768, 2 * c + 1)
    for c in range(4):
        cols += hc(1280, 2 * c) + hc(1280, 2 * c + 1)
    cols += list(range(640, 768))
    cols += list(range(1792, 2304))
    return np.asarray(cols, dtype=np.int64)


def _consts():
    inv = (10000.0 ** (-np.arange(0, 64, 2, dtype=np.float32) / np.float32(64))).astype(np.float32)
    ang = (np.arange(SEQ_S, dtype=np.float32)[:, None] * inv[None, :]).astype(np.float32)
    cos = np.cos(ang).astype(np.float32).T
    sin = np.sin(ang).astype(np.float32).T
    c64 = np.concatenate([cos, cos], 0)
    s64 = np.concatenate([-sin, sin], 0)
    c128 = np.concatenate([c64, c64], 0)
    s128 = np.concatenate([s64, s64], 0)
    rope = np.stack([c128 * np.float32(0.125), s128 * np.float32(0.125), c128, s128], 0).astype(np.float32)
    i = np.arange(128)[:, None]
    j = np.arange(384)[None, :]
    maska = np.where((j >= i) & (j <= i + 256), 0.0, NEGM).astype(np.float32)
    return np.ascontiguousarray(rope), maska


def _bias_table(rpb):
    p = np.arange(128)
    half = p // 64
    c = p % 64
    c0 = np.clip(c - 8, 0, 48)
    jj = np.arange(16)
    kc = np.arange(64)
    ri = jj[None, :] - half[:, None]
    rvalid = (ri >= 0) & (ri <= 14)
    cvalid = (kc[None, :] >= c0[:, None]) & (kc[None, :] < c0[:, None] + 16)
    cidx = np.clip(kc[None, :] - c[:, None] + 15, 0, 30)
    ric = np.clip(ri, 0, 14)
    g = rpb[:, ric[:, :, None], cidx[:, None, :]]
    valid = rvalid[:, :, None] & cvalid[:, None, :]
    tb = np.where(valid[None], g, np.float32(NEGM)).astype(np.float32)
    return np.ascontiguousarray(tb.transpose(1, 0, 2, 3).reshape(128, 8 * 16 * 64))


def _bias_table_interior(tb):
    t = tb.reshape(128, 8, 16, 64)[:, :, 3:12, :].copy()
    t[0:64, :, 8, :] = np.float32(NEGM)
    t[64:128, :, 0, :] = np.float32(NEGM)
    return np.ascontiguousarray(t.reshape(128, 8 * 9 * 64))


_CACHE = {}


def kernel(x_prompt, x_sample, ffn1_pre, ffn1_w_gu, ffn1_w_down, ffn1_post, mix_pre, w_in, sink_a, rpb_b,
           out_norm_a, out_norm_b, w_out, mix_post, ffn2_pre, ffn2_w_gu, ffn2_w_down, ffn2_post, final_norm):
    f = lambda a: np.ascontiguousarray(np.asarray(a, dtype=np.float32))
    x_prompt = f(x_prompt)
    x_sample = f(x_sample)
    if "nc" not in _CACHE:
        _CACHE["nc"] = build_program()
        _CACHE["consts"] = _consts()
    nc = _CACHE["nc"]
    rope, maska = _CACHE["consts"]

    def pk(g):
        return f(g).reshape(8, 128).T

    gpre = np.ascontiguousarray(np.concatenate([pk(ffn1_pre[0]), pk(mix_pre[0]), pk(ffn2_pre[0])], axis=1))
    gpost = np.ascontiguousarray(np.stack([np.broadcast_to(f(g[0])[None, :], (128, D)) for g in (ffn1_post, mix_post, ffn2_post, final_norm)], 0))
    gnab = np.ascontiguousarray(np.broadcast_to(np.concatenate([f(out_norm_a[0]), f(out_norm_b[0])])[None, :], (128, D)))
    sink = np.ascontiguousarray(np.broadcast_to(f(sink_a[0])[None, :], (128, 8)))
    tb = _bias_table(f(rpb_b[0]))
    tbi = _bias_table_interior(tb)
    win = np.ascontiguousarray(f(w_in[0])[:, _win_cols()])
    shared = {
        "w1gu": f(ffn1_w_gu[0]), "w1d": f(ffn1_w_down[0]), "w2gu": f(ffn2_w_gu[0]), "w2d": f(ffn2_w_down[0]),
        "win": win, "wout": f(w_out[0]), "gpre": gpre, "gpost": gpost, "gnab": gnab, "sink": sink, "tb": tb, "tbi": tbi,
        "maska": maska, "rope": rope,
    }
    in_maps = []
    for c in range(8):
        m = dict(shared)
        m["x"] = np.ascontiguousarray(np.concatenate([x_prompt[c], x_sample[c]], axis=0))
        in_maps.append(m)
    res = run_bass_kernel_spmd(nc, in_maps, core_ids=list(range(8)))
    yp = np.stack([res.results[c]["y"][:SEQ_P] for c in range(8)], 0).astype(np.float32)
    ys = np.stack([res.results[c]["y"][SEQ_P:] for c in range(8)], 0).astype(np.float32)
    return (yp, ys)
```

```python
import contextlib
import numpy as np
import concourse.bass as bass
import concourse.mybir as mybir
from concourse.bass_utils import run_bass_kernel_spmd

F32 = mybir.dt.float32
BF16 = mybir.dt.bfloat16
AF = mybir.ActivationFunctionType
ALU = mybir.AluOpType
AX = mybir.AxisListType

D = 1024
DFF = 2816
NJ = DFF // 128
SEQ_P = 2048
SEQ_S = 8192
NTOK = SEQ_P + SEQ_S
EPS = 1e-6
NEGM = -30000.0
WIN_COLS = 2944
RING = 1024
T2 = 256
SW = 1408


class Buf:
    __slots__ = ("writer", "readers")

    def __init__(self):
        self.writer = None
        self.readers = []


class Op:
    __slots__ = ("eng", "idx", "fn", "waits", "needs_inc", "is_dma", "dsem", "dval", "incval")

    def __init__(self, eng, idx, fn, is_dma=False):
        self.eng = eng
        self.idx = idx
        self.fn = fn
        self.waits = []
        self.needs_inc = False
        self.is_dma = is_dma
        self.dsem = None
        self.dval = 0
        self.incval = 0


class Prog:
    ENGS = ("pe", "act", "dve", "pool", "sp")

    def __init__(self, nc, n_dma_sems=16):
        self.nc = nc
        self.streams = {e: [] for e in self.ENGS}
        self.seen = {e: {} for e in self.ENGS}
        self.n_dma_sems = n_dma_sems
        self.dma_rr = 0
        self.dma_last = [None] * n_dma_sems
        self.dma_cnt = [0] * n_dma_sems
        self.bufs = {}
        self.last_real = {e: None for e in self.ENGS}

    def buf(self, name):
        b = self.bufs.get(name)
        if b is None:
            b = Buf()
            self.bufs[name] = b
        return b

    def _dep(self, op, prod):
        if prod is None:
            return
        e = op.eng
        if prod.is_dma:
            key = ("d", prod.dsem)
            val = prod.dval
        else:
            key = prod.eng
            val = prod.idx
        if self.seen[e].get(key, -1) >= val:
            return
        self.seen[e][key] = val
        prod.needs_inc = True
        op.waits.append(prod)

    def op(self, eng, fn, reads=(), writes=(), is_dma=False):
        st = self.streams[eng]
        o = Op(eng, len(st), fn, is_dma)
        if is_dma:
            s = self.dma_rr
            self.dma_rr = (s + 1) % self.n_dma_sems
            prev = self.dma_last[s]
            o.dsem = s
            self.dma_cnt[s] += 1
            o.dval = self.dma_cnt[s]
            if prev is not None:
                self._dep(o, prev)
            self.dma_last[s] = o
            o.needs_inc = True
        rb = [self.buf(b) for b in reads]
        wb = [self.buf(b) for b in writes]
        for b in rb:
            self._dep(o, b.writer)
        for b in wb:
            self._dep(o, b.writer)
            for r in b.readers:
                self._dep(o, r)
        for b in rb:
            b.readers.append(o)
        for b in wb:
            b.writer = o
            b.readers = []
        st.append(o)
        if fn is not None and not is_dma:
            self.last_real[eng] = o
        return o

    def pe(self, fn, reads=(), writes=()):
        return self.op("pe", fn, reads, writes)

    def act(self, fn, reads=(), writes=()):
        return self.op("act", fn, reads, writes)

    def dve(self, fn, reads=(), writes=()):
        return self.op("dve", fn, reads, writes)

    def pool(self, fn, reads=(), writes=()):
        return self.op("pool", fn, reads, writes)

    def dma(self, fn, reads=(), writes=()):
        return self.op("sp", fn, reads, writes, is_dma=True)

    def barrier(self):
        prods = [self.last_real[e] for e in ("pe", "act", "dve", "pool")]
        prods += list(self.dma_last)
        for e in self.ENGS:
            o = Op(e, len(self.streams[e]), None)
            for p in prods:
                if p is not None:
                    self._dep(o, p)
            self.streams[e].append(o)
        self.bufs = {}

    def emit(self):
        nc = self.nc
        self.barrier()
        for e in self.ENGS:
            c = 0
            for o in self.streams[e]:
                if o.is_dma:
                    o.incval = 16 * o.dval
                elif o.needs_inc:
                    c += 1
                    o.incval = c
        with contextlib.ExitStack() as es:
            esem = {e: es.enter_context(nc.semaphore("s_" + e)) for e in self.ENGS}
            dsem = [es.enter_context(nc.semaphore("d%d" % i)) for i in range(self.n_dma_sems)]
            block = es.enter_context(nc.Block())

            def run(e, eng):
                for o in self.streams[e]:
                    for p in o.waits:
                        if p.is_dma:
                            eng.wait_ge(dsem[p.dsem], p.incval)
                        else:
                            eng.wait_ge(esem[p.eng], p.incval)
                    if o.fn is None:
                        continue
                    ins = o.fn(eng)
                    if o.is_dma:
                        ins.then_inc(dsem[o.dsem], 16)
                    elif o.needs_inc:
                        ins.then_inc(esem[e], 1)

            @block.tensor
            def _(eng):
                run("pe", eng)

            @block.scalar
            def _(eng):
                run("act", eng)

            @block.vector
            def _(eng):
                run("dve", eng)

            @block.gpsimd
            def _(eng):
                run("pool", eng)

            @block.sync
            def _(eng):
                run("sp", eng)


class Arena:
    def __init__(self, t_f32, nwords):
        self.f = t_f32
        self.b = t_f32.bitcast(BF16)
        self.off = 0
        self.cap = nwords * 4

    def alloc(self, n, dt):
        self.off = (self.off + 63) // 64 * 64
        if dt == F32:
            ap = self.f[:, self.off // 4: self.off // 4 + n]
            self.off += 4 * n
        else:
            ap = self.b[:, self.off // 2: self.off // 2 + n]
            self.off += 2 * n
        assert self.off <= self.cap, ("SBUF arena overflow", self.off, self.cap)
        return ap


class Rot:
    def __init__(self, items):
        self.items = items
        self.i = 0

    def next(self):
        it = self.items[self.i % len(self.items)]
        self.i += 1
        return it


def load_weight(P, stg, wdram, KC, N, dst, gain, eng_rr):
    for k in range(KC):
        for n0 in range(0, N, SW):
            n1 = min(N, n0 + SW)
            sap, sname = stg.next()
            P.dma(lambda e, sap=sap, k=k, n0=n0, n1=n1: e.dma_start(out=sap[:, 0:n1 - n0], in_=wdram[k * 128:(k + 1) * 128, n0:n1]),
                  writes=[sname])
            which = eng_rr[0] % 2
            eng_rr[0] += 1
            o = dst[:, k, n0:n1]
            i = sap[:, 0:n1 - n0]
            if gain is not None:
                gk = gain[:, k:k + 1]
                if which == 0:
                    P.dve(lambda e, o=o, i=i, gk=gk: e.tensor_scalar(out=o, in0=i, scalar1=gk, scalar2=None, op0=ALU.mult), reads=[sname])
                elif which == 1:
                    P.act(lambda e, o=o, i=i, gk=gk: e.activation(out=o, in_=i, func=AF.Copy, scale=gk), reads=[sname])
                else:
                    P.pool(lambda e, o=o, i=i, gk=gk: e.tensor_scalar(out=o, in0=i, scalar1=gk, scalar2=None, op0=ALU.mult), reads=[sname])
            else:
                if which == 0:
                    P.dve(lambda e, o=o, i=i: e.tensor_copy(out=o, in_=i), reads=[sname])
                elif which == 1:
                    P.act(lambda e, o=o, i=i: e.copy(out=o, in_=i), reads=[sname])
                else:
                    P.pool(lambda e, o=o, i=i: e.tensor_copy(out=o, in_=i), reads=[sname])


def build_program(debug=False, phases=(1, 2, 3)):
    nc = bass.Bass("TRN2", target_bir_lowering=False)

    def din(name, shape):
        return nc.dram_tensor(name, list(shape), F32, kind="ExternalInput").ap()

    x_d = din("x", [NTOK, D])
    w1gu_d = din("w1gu", [D, 2 * DFF])
    w1d_d = din("w1d", [DFF, D])
    w2gu_d = din("w2gu", [D, 2 * DFF])
    w2d_d = din("w2d", [DFF, D])
    win_d = din("win", [D, WIN_COLS])
    wout_d = din("wout", [D, D])
    gpre_d = din("gpre", [128, 24])
    gpost_d = din("gpost", [4, 128, D])
    gnab_d = din("gnab", [128, D])
    sink_d = din("sink", [128, 8])
    tb_d = din("tb", [128, 8 * 16 * 64])
    maska_d = din("maska", [128, 384])
    rope_d = din("rope", [4, 128, SEQ_S])
    y_d = nc.dram_tensor("y", [NTOK, D], F32, kind="ExternalOutput").ap()
    skind = "ExternalOutput" if debug else "Internal"
    xs1_d = nc.dram_tensor("xs1", [NTOK, D], F32, kind=skind).ap()
    xs2_d = nc.dram_tensor("xs2", [NTOK, D], F32, kind=skind).ap()

    with contextlib.ExitStack() as es:
        NW = 53200
        arena_t = es.enter_context(nc.sbuf_tensor("arena", [128, NW], F32))
        ps_t = es.enter_context(nc.psum_tensor("ps", [128, 4096], F32))
        psf = ps_t[:]
        psb = ps_t[:].bitcast(BF16)
        P = Prog(nc)
        eng_rr = [0]

        def bankf(i, n=512):
            return psf[:, i * 512:i * 512 + n]

        def bankb(i, n=1024):
            return psb[:, i * 1024:i * 1024 + n]

        def make_ident(A):
            identf = A.alloc(128, F32)
            ident = A.alloc(128, BF16)
            P.pool(lambda e: e.memset(identf, 0.0), writes=["identf"])
            P.pool(lambda e: e.affine_select(out=identf, in_=identf, pattern=[[-1, 128]], compare_op=ALU.not_equal,
                                             fill=1.0, base=0, channel_multiplier=1), reads=["identf"], writes=["identf"])
            P.dve(lambda e: e.tensor_copy(out=ident, in_=identf), reads=["identf"], writes=["ident"])
            return identf, ident

        def ffn_phase(src_d, dst_d, wgu_d, wd_d, gidx, pidx, final):
            A = Arena(arena_t[:], NW)
            wgu = A.alloc(8 * 2 * DFF, BF16).rearrange("p (k n) -> p k n", k=8)
            wd = A.alloc(NJ * D, BF16).rearrange("p (k n) -> p k n", k=NJ)
            gpre = A.alloc(8, F32)
            gpost = A.alloc(D, F32)
            gfin = A.alloc(D, F32) if final else None
            epst = A.alloc(1, F32)
            identf, ident = make_ident(A)
            P.pool(lambda e: e.memset(epst, EPS), writes=["epst"])
            P.dma(lambda e: e.dma_start(out=gpre, in_=gpre_d[:, gidx * 8:(gidx + 1) * 8]), writes=["gpre"])
            P.dma(lambda e: e.dma_start(out=gpost, in_=gpost_d[pidx]), writes=["gpost"])
            P.dve(lambda e: e.tensor_scalar(out=gpost, in0=gpost, scalar1=0.5, scalar2=None, op0=ALU.mult), reads=["gpost"], writes=["gpost"])
            if final:
                P.dma(lambda e: e.dma_start(out=gfin, in_=gpost_d[3]), writes=["gfin"])
            mark = A.off
            stg = Rot([(A.alloc(SW, F32), "stg%d" % i) for i in range(3)])
            P.barrier()
            load_weight(P, stg, wgu_d, 8, 2 * DFF, wgu, gpre, eng_rr)
            load_weight(P, stg, wd_d, NJ, D, wd, None, eng_rr)
            P.barrier()
            A.off = mark
            xt = Rot([(A.alloc(D, F32), "xt%d" % i) for i in range(2)])
            xn = [(A.alloc(D, BF16), "xn%d" % i) for i in range(4)]
            xnT = A.alloc(8 * 512, BF16).rearrange("p (k n) -> p k n", k=8)
            h2T = A.alloc(NJ * 512, BF16).rearrange("p (k n) -> p k n", k=NJ)
            sg = Rot([(A.alloc(512, BF16), "sg%d" % i) for i in range(2)])
            xres = Rot([(A.alloc(D, F32), "xres%d" % i) for i in range(2)])
            tt = Rot([(A.alloc(D, F32), "tt%d" % i) for i in range(2)])
            ss4 = Rot([(A.alloc(4, F32), "ss4_%d" % i) for i in range(2)])
            rs4 = Rot([(A.alloc(4, F32), "rs4_%d" % i) for i in range(2)])
            sp1 = Rot([(A.alloc(2, F32), "sp1_%d" % i) for i in range(3)])
            sf1 = Rot([(A.alloc(2, F32), "sf1_%d" % i) for i in range(3)])
            gub = Rot([(i, "pb%d" % i) for i in range(4)])
            dbank = Rot([(psf[:, 2048:3072], "pd0"), (psf[:, 3072:4096], "pd1")])
            ntile = NTOK // 512

            cur = {}

            def prep_load(t, pair):
                if pair == 0:
                    cur["s4"] = ss4.next()
                    cur["r4"] = rs4.next()
                    cur["subs"] = {}
                s4, s4n = cur["s4"]
                for s in (2 * pair, 2 * pair + 1):
                    xa, xname = xt.next()
                    xo, xon = xn[s]
                    r0 = t * 512 + s * 128
                    P.dma(lambda e, xa=xa, r0=r0: e.dma_start(out=xa, in_=src_d[r0:r0 + 128, :]), writes=[xname])
                    P.dve(lambda e, xa=xa, s=s, s4=s4, xo=xo: e.scalar_tensor_tensor(out=xo, in0=xa, scalar=1.0, in1=xa, op0=ALU.mult, op1=ALU.mult,
                                                                                       accum_out=s4[:, s:s + 1]),
                          reads=[xname], writes=[xon, s4n + "_%d" % s])
                    cur["subs"][s] = (xa, xname)

            def prep_sqrt(t, pair):
                s4, s4n = cur["s4"]
                r4, r4n = cur["r4"]
                lo = 2 * pair
                P.act(lambda e, s4=s4, lo=lo: e.activation(out=s4[:, lo:lo + 2], in_=s4[:, lo:lo + 2], func=AF.Sqrt, scale=1.0 / D, bias=epst),
                      reads=[s4n + "_%d" % lo, s4n + "_%d" % (lo + 1), "epst"], writes=[s4n + "p%d" % lo])
                P.dve(lambda e, s4=s4, r4=r4, lo=lo: e.reciprocal(out=r4[:, lo:lo + 2], in_=s4[:, lo:lo + 2]),
                      reads=[s4n + "p%d" % lo], writes=[r4n + "p%d" % lo])

            def prep_scale(t, pair):
                r4, r4n = cur["r4"]
                lo = 2 * pair
                for q in (lo, lo + 1):
                    xa2, xname2 = cur["subs"][q]
                    xo, xon = xn[q]
                    P.act(lambda e, xo=xo, xa2=xa2, r4=r4, q=q: e.activation(out=xo, in_=xa2, func=AF.Copy, scale=r4[:, q:q + 1]),
                          reads=[xname2, r4n + "p%d" % lo], writes=[xon])

            def prep_T(s):
                xo, xon = xn[s]
                bi, bn = gub.next()

                def tr(e):
                    for k in range(8):
                        i = e.transpose(out=bankb(bi)[:, k * 128:(k + 1) * 128], in_=xo[:, k * 128:(k + 1) * 128], identity=ident)
                    return i
                P.pe(tr, reads=[xon, "ident"], writes=[bn])
                P.dve(lambda e, s=s: e.tensor_copy(out=xnT[:, :, s * 128:(s + 1) * 128], in_=bankb(bi).rearrange("p (k n) -> p k n", k=8)),
                      reads=[bn], writes=["xnT"])

            def gu(j):
                gi, gbn = gub.next()
                gb = bankf(gi)
                sga, sgn = sg.next()

                def mmg(e):
                    for k in range(8):
                        i = e.matmul(gb, lhsT=wgu[:, k, j * 128:(j + 1) * 128], rhs=xnT[:, k, :], start=(k == 0), stop=(k == 7))
                    return i
                P.pe(mmg, reads=["xnT"], writes=[gbn])
                P.act(lambda e: e.activation(out=sga, in_=gb, func=AF.Silu), reads=[gbn], writes=[sgn])
                ui, ubn = gub.next()
                ub = bankf(ui)

                def mmu(e):
                    for k in range(8):
                        i = e.matmul(ub, lhsT=wgu[:, k, DFF + j * 128:DFF + (j + 1) * 128], rhs=xnT[:, k, :], start=(k == 0), stop=(k == 7))
                    return i
                P.pe(mmu, reads=["xnT"], writes=[ubn])
                P.dve(lambda e: e.tensor_tensor(out=h2T[:, j, :], in0=ub, in1=sga, op=ALU.mult), reads=[ubn, sgn], writes=["h2T"])

            def down(t, s):
                db, dbn = dbank.next()

                def mm(e):
                    for n in range(2):
                        for j in range(NJ):
                            i = e.matmul(db[:, n * 512:(n + 1) * 512], lhsT=h2T[:, j, s * 128:(s + 1) * 128], rhs=wd[:, j, n * 512:(n + 1) * 512],
                                         start=(j == 0), stop=(j == NJ - 1))
                    return i
                P.pe(mm, reads=["h2T"], writes=[dbn])
                r0 = t * 512 + s * 128
                xr, xrn = xres.next()
                ta, tan = tt.next()
                s1, s1n = sp1.next()
                f1, f1n = sf1.next()
                P.dma(lambda e: e.dma_start(out=xr, in_=src_d[r0:r0 + 128, :]), writes=[xrn])
                P.act(lambda e: e.activation(out=ta, in_=db, func=AF.Square, accum_out=s1[:, 0:1]), reads=[dbn], writes=[tan, s1n])
                P.act(lambda e: e.activation(out=s1[:, 0:1], in_=s1[:, 0:1], func=AF.Sqrt, scale=1.0 / D, bias=epst), reads=[s1n, "epst"], writes=[s1n])
                P.dve(lambda e: e.reciprocal(out=s1[:, 1:2], in_=s1[:, 0:1]), reads=[s1n], writes=[s1n])
                P.dve(lambda e: e.scalar_tensor_tensor(out=ta, in0=db, scalar=s1[:, 1:2], in1=gpost, op0=ALU.mult, op1=ALU.mult),
                      reads=[dbn, s1n, "gpost"], writes=[tan])
                if not final:
                    P.pool(lambda e: e.tensor_tensor(out=xr, in0=xr, in1=ta, op=ALU.add), reads=[xrn, tan], writes=[xrn])
                    P.dma(lambda e: e.dma_start(out=dst_d[r0:r0 + 128, :], in_=xr), reads=[xrn], writes=["out%d" % r0])
                else:
                    P.pool(lambda e: e.tensor_tensor(out=xr, in0=xr, in1=ta, op=ALU.add), reads=[xrn, tan], writes=[xrn])
                    P.dve(lambda e: e.scalar_tensor_tensor(out=ta, in0=xr, scalar=1.0, in1=xr, op0=ALU.mult, op1=ALU.mult, accum_out=f1[:, 0:1]),
                          reads=[xrn], writes=[tan, f1n])
                    P.act(lambda e: e.activation(out=f1[:, 0:1], in_=f1[:, 0:1], func=AF.Sqrt, scale=1.0 / D, bias=epst), reads=[f1n, "epst"], writes=[f1n])
                    P.dve(lambda e: e.reciprocal(out=f1[:, 1:2], in_=f1[:, 0:1]), reads=[f1n], writes=[f1n])
                    P.dve(lambda e: e.scalar_tensor_tensor(out=ta, in0=xr, scalar=f1[:, 1:2], in1=gfin, op0=ALU.mult, op1=ALU.mult),
                          reads=[xrn, f1n, "gfin"], writes=[tan])
                    P.dma(lambda e: e.dma_start(out=dst_d[r0:r0 + 128, :], in_=ta), reads=[tan], writes=["out%d" % r0])

            for pair in range(2):
                prep_load(0, pair)
                prep_sqrt(0, pair)
                prep_scale(0, pair)
            for s in range(4):
                prep_T(s)
            sched = {1: (prep_load, 0), 4: (prep_sqrt, 0), 5: (prep_scale, 0), 8: (prep_load, 1), 11: (prep_sqrt, 1), 12: (prep_scale, 1)}
            for t in range(ntile):
                for j in range(NJ):
                    gu(j)
                    if j in sched and t + 1 < ntile:
                        fn, pair = sched[j]
                        fn(t + 1, pair)
                for s in range(4):
                    if t + 1 < ntile:
                        prep_T(s)
                    down(t, s)
            P.barrier()

        def mixer_phase():
            A = Arena(arena_t[:], NW)
            win = A.alloc(8 * WIN_COLS, BF16).rearrange("p (k n) -> p k n", k=8)
            wout = A.alloc(8 * D, BF16).rearrange("p (k n) -> p k n", k=8)
            tb = A.alloc(8 * 16 * 64, BF16)
            maska = A.alloc(384, BF16)
            negt = A.alloc(64, BF16)
            gmix = A.alloc(8, F32)
            gpost = A.alloc(D, F32)
            gnab = A.alloc(D, F32)
            sink8 = A.alloc(8, F32)
            epst = A.alloc(1, F32)
            identf, ident = make_ident(A)
            idh0 = A.alloc(128, BF16)
            idh1 = A.alloc(128, BF16)
            P.pool(lambda e: e.memset(epst, EPS), writes=["epst"])
            P.pool(lambda e: e.memset(negt, NEGM), writes=["negt"])
            P.pool(lambda e: e.memset(idh0, 0.0), writes=["idh0"])
            P.pool(lambda e: e.memset(idh1, 0.0), writes=["idh1"])
            P.dve(lambda e: e.tensor_copy(out=idh0[0:64, :], in_=identf[0:64, :]), reads=["identf", "idh0"], writes=["idh0"])
            P.dve(lambda e: e.tensor_copy(out=idh1[64:128, :], in_=identf[64:128, :]), reads=["identf", "idh1"], writes=["idh1"])
            P.dma(lambda e: e.dma_start(out=gmix, in_=gpre_d[:, 8:16]), writes=["gmix"])
            P.dma(lambda e: e.dma_start(out=gpost, in_=gpost_d[1]), writes=["gpost"])
            P.dma(lambda e: e.dma_start(out=gnab, in_=gnab_d), writes=["gnab"])
            P.dma(lambda e: e.dma_start(out=sink8, in_=sink_d), writes=["sink8"])
            mark = A.off
            stg = Rot([(A.alloc(SW, F32), "stg%d" % i) for i in range(3)])
            P.barrier()
            load_weight(P, stg, win_d, 8, WIN_COLS, win, gmix, eng_rr)
            load_weight(P, stg, wout_d, 8, D, wout, None, eng_rr)
            for n0 in range(0, 8 * 16 * 64, SW):
                n1 = min(8 * 16 * 64, n0 + SW)
                sap, sname = stg.next()
                P.dma(lambda e, sap=sap, n0=n0, n1=n1: e.dma_start(out=sap[:, 0:n1 - n0], in_=tb_d[:, n0:n1]), writes=[sname])
                P.dve(lambda e, sap=sap, n0=n0, n1=n1: e.tensor_copy(out=tb[:, n0:n1], in_=sap[:, 0:n1 - n0]), reads=[sname])
            sap, sname = stg.next()
            P.dma(lambda e, sap=sap: e.dma_start(out=sap[:, 0:384], in_=maska_d), writes=[sname])
            P.dve(lambda e, sap=sap: e.tensor_copy(out=maska, in_=sap[:, 0:384]), reads=[sname])
            P.barrier()
            A.off = mark
            tbv = tb.rearrange("p (h j c) -> p h j c", h=8, j=16)
            NBLK = RING // 128
            KaT = A.alloc(RING, BF16)
            Va = A.alloc(NBLK * 128, BF16).rearrange("p (b n) -> p b n", b=NBLK)
            KbT = A.alloc(4 * RING, BF16).rearrange("p (c n) -> p c n", c=4)
            Vb = A.alloc(NBLK * 512, BF16).rearrange("p (b n) -> p b n", b=NBLK)
            QaT = [A.alloc(4 * T2, BF16).rearrange("p (c n) -> p c n", c=4) for _ in range(3)]
            QbT = [A.alloc(4 * T2, BF16).rearrange("p (c n) -> p c n", c=4) for _ in range(3)]
            xnT = A.alloc(8 * T2, BF16).rearrange("p (k n) -> p k n", k=8)
            xt = Rot([(A.alloc(D, F32), "xt%d" % i) for i in range(2)])
            xn = [(A.alloc(D, BF16), "xn%d" % i) for i in range(2)]
            ropeb = Rot([(A.alloc(4 * T2, F32).rearrange("p (f n) -> p f n", f=4), "rope%d" % i) for i in range(2)])
            t1 = Rot([(A.alloc(T2, F32), "t1_%d" % i) for i in range(2)])
            t2 = Rot([(A.alloc(T2, F32), "t2_%d" % i) for i in range(2)])
            Pa = Rot([(A.alloc(384, BF16), "Pa%d" % i) for i in range(2)])
            PTa = Rot([(A.alloc(384, BF16), "PTa%d" % i) for i in range(2)])
            Pb = Rot([(A.alloc(576, BF16), "Pb%d" % i) for i in range(2)])
            PTb = Rot([(A.alloc(640, BF16), "PTb%d" % i) for i in range(2)])
            oab = Rot([(A.alloc(D, F32), "oab%d" % i) for i in range(2)])
            on = Rot([(A.alloc(D, BF16), "on%d" % i) for i in range(2)])
            onT = Rot([(A.alloc(D, BF16).rearrange("p (k n) -> p k n", k=8), "onT%d" % i) for i in range(2)])
            xres = Rot([(A.alloc(D, F32), "xres%d" % i) for i in range(2)])
            tt = Rot([(A.alloc(D, F32), "tt%d" % i) for i in range(2)])
            st2 = Rot([(A.alloc(4, F32), "st2_%d" % i) for i in range(2)])
            sta = Rot([(A.alloc(40, F32), "sta%d" % i) for i in range(4)])
            stb = Rot([(A.alloc(24, F32), "stb%d" % i) for i in range(4)])
            sn = Rot([(A.alloc(4, F32), "sn%d" % i) for i in range(6)])
            wb = Rot([(0, "pw0")])
            abank = Rot([(1, "pA1"), (2, "pA2")])

            def ring_pieces(tok0, n):
                out = []
                off = 0
                while off < n:
                    c = (tok0 + off) % RING
                    ln = min(n - off, RING - c)
                    out.append((c, ln, off))
                    off += ln
                return out

            def stage_a_gen(seq0, S, i, g):
                tok0 = i * T2
                sl = g % 3
                s2, s2n = st2.next()
                subs = []
                for s in range(2):
                    xa, xname = xt.next()
                    r0 = seq0 + tok0 + s * 128
                    P.dma(lambda e, xa=xa, r0=r0: e.dma_start(out=xa, in_=xs1_d[r0:r0 + 128, :]), writes=[xname])
                    P.dve(lambda e, xa=xa, s=s: e.scalar_tensor_tensor(out=xn[s][0], in0=xa, scalar=1.0, in1=xa, op0=ALU.mult, op1=ALU.mult,
                                                                        accum_out=s2[:, s:s + 1]),
                          reads=[xname], writes=[xn[s][1], s2n + "_%d" % s])
                    subs.append((xa, xname))
                rp, rpn = ropeb.next()
                P.dma(lambda e: e.dma_start(out=rp, in_=rope_d[:, :, tok0:tok0 + T2].rearrange("f p n -> p f n")), writes=[rpn])
                yield
                P.act(lambda e: e.activation(out=s2[:, 0:2], in_=s2[:, 0:2], func=AF.Ln, scale=1.0 / D, bias=epst),
                      reads=[s2n + "_0", s2n + "_1", "epst"], writes=[s2n + "p"])
                P.act(lambda e: e.activation(out=s2[:, 2:4], in_=s2[:, 0:2], func=AF.Exp, scale=-0.5), reads=[s2n + "p"], writes=[s2n + "r"])
                yield
                for s in range(2):
                    xa, xname = subs[s]
                    xo, xon = xn[s]
                    P.act(lambda e, xo=xo, xa=xa, s=s: e.activation(out=xo, in_=xa, func=AF.Copy, scale=s2[:, 2 + s:3 + s]),
                          reads=[xname, s2n + "r"], writes=[xon])
                yield
                for s in range(2):
                    xo, xon = xn[s]
                    bi, bn = wb.next()

                    def tr(e, xo=xo, bi=bi):
                        for k in range(8):
                            ins = e.transpose(out=bankb(bi)[:, k * 128:(k + 1) * 128], in_=xo[:, k * 128:(k + 1) * 128], identity=ident)
                        return ins
                    P.pe(tr, reads=[xon, "ident"], writes=[bn])
                    P.dve(lambda e, s=s, bi=bi: e.tensor_copy(out=xnT[:, :, s * 128:(s + 1) * 128], in_=bankb(bi).rearrange("p (k n) -> p k n", k=8)),
                          reads=[bn], writes=["xnT"])
                    yield
                rc = tok0 % RING
                kblk = ["kv%d" % ((tok0 // 128 + s) % NBLK) for s in range(2)]

                def wmm(e, bi, col0, half):
                    for k in range(8):
                        ins = e.matmul(bankf(bi)[:, half * T2:(half + 1) * T2], lhsT=win[:, k, col0:col0 + 128], rhs=xnT[:, k, :],
                                       start=(k == 0), stop=(k == 7))
                    return ins
                for c in range(5):
                    bi, bn = wb.next()
                    pc, sc = (c, 4 + c) if c < 4 else (8, 9)

                    def mm(e, bi=bi, pc=pc, sc=sc):
                        wmm(e, bi, pc * 128, 0)
                        return wmm(e, bi, sc * 128, 1)
                    P.pe(mm, reads=["xnT"], writes=[bn])
                    ta, tan = t1.next()
                    tb2, tbn = t2.next()
                    f0 = 0 if c < 4 else 2
                    P.dve(lambda e, bi=bi, ta=ta, f0=f0: e.tensor_tensor(out=ta, in0=bankf(bi)[:, 0:T2], in1=rp[:, f0, :], op=ALU.mult),
                          reads=[bn, rpn], writes=[tan])
                    P.dve(lambda e, bi=bi, tb2=tb2, f0=f0: e.tensor_tensor(out=tb2, in0=bankf(bi)[:, T2:2 * T2], in1=rp[:, f0 + 1, :], op=ALU.mult),
                          reads=[bn, rpn], writes=[tbn])
                    if c < 4:
                        dst = QaT[sl][:, c, :]
                        wn = ["QaT%d" % sl]
                    else:
                        dst = KaT[:, rc:rc + T2]
                        wn = ["ka" + b for b in kblk]
                    P.pool(lambda e, dst=dst, ta=ta, tb2=tb2: e.tensor_tensor(out=dst, in0=ta, in1=tb2, op=ALU.add), reads=[tan, tbn], writes=wn)
                    yield
                for c in range(4):
                    bi, bn = wb.next()

                    def mm(e, bi=bi, c=c):
                        wmm(e, bi, (10 + c) * 128, 0)
                        return wmm(e, bi, (14 + c) * 128, 1)
                    P.pe(mm, reads=["xnT"], writes=[bn])
                    P.act(lambda e, bi=bi, c=c: e.activation(out=QbT[sl][:, c, :], in_=bankf(bi)[:, 0:T2], func=AF.Copy, scale=0.125),
                          reads=[bn], writes=["QbT%d" % sl])
                    P.act(lambda e, bi=bi, c=c: e.copy(out=KbT[:, c, rc:rc + T2], in_=bankf(bi)[:, T2:2 * T2]),
                          reads=[bn], writes=["kb" + b for b in kblk])
                    yield
                for s in range(2):
                    blk = (tok0 // 128 + s) % NBLK
                    bi, bn = wb.next()

                    def mmva(e, bi=bi, s=s):
                        for k in range(8):
                            ins = e.matmul(bankf(bi)[:, 0:128], lhsT=xnT[:, k, s * 128:(s + 1) * 128], rhs=win[:, k, 2304:2432], start=(k == 0), stop=(k == 7))
                        return ins
                    P.pe(mmva, reads=["xnT"], writes=[bn])
                    P.act(lambda e, bi=bi, blk=blk: e.copy(out=Va[:, blk, :], in_=bankf(bi)[:, 0:128]), reads=[bn], writes=["va" + kblk[s]])
                    yield
                    bi2, bn2 = wb.next()

                    def mmvb(e, bi2=bi2, s=s):
                        for k in range(8):
                            ins = e.matmul(bankf(bi2), lhsT=xnT[:, k, s * 128:(s + 1) * 128], rhs=win[:, k, 2432:2944], start=(k == 0), stop=(k == 7))
                        return ins
                    P.pe(mmvb, reads=["xnT"], writes=[bn2])
                    P.dve(lambda e, bi2=bi2, blk=blk: e.tensor_copy(out=Vb[:, blk, :], in_=bankf(bi2)), reads=[bn2], writes=["vb" + kblk[s]])
                    yield

            class Unit:
                pass

            def unit_qk(kind, S, i, g, u, h, stat, statn):
                U = Unit()
                sl = g % 3
                qtok0 = i * T2 + u * 128
                U.kind, U.h, U.stat, U.statn = kind, h, stat, statn
                if kind == "a":
                    nb = S // 128
                    b = qtok0 // 128
                    kb0 = max(b - 1, 0)
                    kb1 = min(b + 1, nb - 1)
                    nk = (kb1 - kb0 + 1) * 128
                    ktok0 = kb0 * 128
                    pb = 0 if h < 4 else 64
                    lhsT = QaT[sl][pb:pb + 64, h % 4, u * 128:(u + 1) * 128]
                    qn = "QaT%d" % sl
                    abi, abn = abank.next()
                    sbank = psf[:, abi * 512:abi * 512 + nk]
                    sname = [abn]
                    mcol0 = (kb0 - (b - 1)) * 128
                    U.vcol = (h // 4) * 64
                    U.ptv = bankb(abi)[:, 0:384]
                    U.pbuf, U.pbn = Pa.next()
                    U.ptbuf, U.ptn = PTa.next()
                    U.ob = bankf(6)
                    U.obn = "pOa"
                    U.V = Va
                    U.ptname = abn
                    nr = 0
                else:
                    rows = S // 64
                    r = qtok0 // 64
                    if r < 4:
                        kr0, nr = 0, 8
                    elif r >= rows - 4:
                        kr0, nr = rows - 8, 8
                    else:
                        kr0, nr = r - 4, 9
                    nk = nr * 64
                    ktok0 = kr0 * 64
                    pb = (h % 2) * 64
                    lhsT = QbT[sl][pb:pb + 64, h // 2, u * 128:(u + 1) * 128]
                    qn = "QbT%d" % sl
                    sbank = psf[:, 3 * 512:3 * 512 + nk]
                    sname = ["pB"]
                    j0 = kr0 - r + 7
                    U.vcol = h * 64
                    U.ptv = bankb(5)[:, 0:640]
                    U.pbuf, U.pbn = Pb.next()
                    U.ptbuf, U.ptn = PTb.next()
                    U.ob = bankf(7)
                    U.obn = "pOb"
                    U.V = Vb
                    U.ptname = "pPTb"
                U.nk = nk
                U.nfull = nk // 128
                U.rem = nk - U.nfull * 128
                U.kblks = [((ktok0 // 128 + q) % NBLK) for q in range(U.nfull + (1 if U.rem else 0))]
                kpre = "ka" if kind == "a" else "kb"
                vpre = "va" if kind == "a" else "vb"
                kreads = [kpre + "kv%d" % q for q in U.kblks]
                U.vreads = [vpre + "kv%d" % q for q in U.kblks]

                def qk(e):
                    segs = [(0, min(nk, 512))] + ([(512, nk)] if nk > 512 else [])
                    for (a0, a1) in segs:
                        if kind == "a":
                            e.matmul(sbank[:, a0:a1], lhsT=ident, rhs=maska[:, mcol0 + a0:mcol0 + a1], start=True, stop=False)
                        else:
                            rhs = tbv[:, h, j0:j0 + nr, :].rearrange("p j c -> p (j c)")
                            e.matmul(sbank[:, a0:a1], lhsT=ident, rhs=rhs[:, a0:a1], start=True, stop=False)
                    if kind == "b" and nr == 9:
                        e.matmul(sbank[:, 512:576], lhsT=idh0, rhs=negt, start=False, stop=False)
                        e.matmul(sbank[:, 0:64], lhsT=idh1, rhs=negt, start=False, stop=False)
                    pieces = []
                    for (c0, ln, off) in ring_pieces(ktok0, nk):
                        if off < 512 < off + ln:
                            pieces.append((c0, 512 - off, off))
                            pieces.append((c0 + 512 - off, off + ln - 512, 512))
                        else:
                            pieces.append((c0, ln, off))
                    ins = None
                    for pi, (c0, ln, off) in enumerate(pieces):
                        last_in_seg = (pi == len(pieces) - 1) or (pieces[pi + 1][2] >= 512 > off)
                        if kind == "a":
                            rhs = KaT[pb:pb + 64, c0:c0 + ln]
                        else:
                            rhs = KbT[pb:pb + 64, h // 2, c0:c0 + ln]
                        ins = e.matmul(sbank[:, off:off + ln], lhsT=lhsT, rhs=rhs, start=False, stop=last_in_seg)
                    return ins
                P.pe(qk, reads=[qn, "ident"] + kreads, writes=sname)
                P.dve(lambda e: e.tensor_reduce(out=stat[:, h:h + 1], in_=sbank, axis=AX.X, op=ALU.max, negate=True),
                      reads=sname, writes=[statn + "m%d" % h])
                P.act(lambda e: e.activation(out=U.pbuf[:, 0:nk], in_=sbank, func=AF.Exp, bias=stat[:, h:h + 1], scale=1.0, accum_out=stat[:, 8 + h:9 + h]),
                      reads=sname + [statn + "m%d" % h], writes=[U.pbn, statn + "s%d" % h])
                return U

            def unit_trp(U):
                nfull, rem, pbuf, ptv, ptbuf = U.nfull, U.rem, U.pbuf, U.ptv, U.ptbuf

                def trp(e):
                    for q in range(nfull):
                        ins = e.transpose(out=ptv[:, q * 128:(q + 1) * 128], in_=pbuf[:, q * 128:(q + 1) * 128], identity=ident)
                    if rem:
                        ins = e.transpose(out=ptv[0:rem, nfull * 128:(nfull + 1) * 128], in_=pbuf[:, nfull * 128:nfull * 128 + rem], identity=ident)
                    return ins
                P.pe(trp, reads=[U.pbn, "ident"], writes=[U.ptname])
                if U.kind == "a":
                    P.act(lambda e: e.activation(out=ptbuf[:, 0:nfull * 128], in_=ptv[:, 0:nfull * 128], func=AF.Copy), reads=[U.ptname], writes=[U.ptn])
                else:
                    P.dve(lambda e: e.tensor_copy(out=ptbuf[:, 0:nfull * 128], in_=ptv[:, 0:nfull * 128]), reads=[U.ptname], writes=[U.ptn])
                if rem:
                    P.dve(lambda e: e.tensor_copy(out=ptbuf[0:rem, nfull * 128:(nfull + 1) * 128], in_=ptv[0:rem, nfull * 128:(nfull + 1) * 128]),
                          reads=[U.ptname], writes=[U.ptn + "r"])

            def unit_pv(U):
                nfull, rem, ptbuf, h = U.nfull, U.rem, U.ptbuf, U.h

                def pv(e):
                    nq = nfull + (1 if rem else 0)
                    for q in range(nq):
                        kp = 128 if q < nfull else rem
                        ins = e.matmul(U.ob[:, h * 64:(h + 1) * 64], lhsT=ptbuf[0:kp, q * 128:(q + 1) * 128], rhs=U.V[0:kp, U.kblks[q], U.vcol:U.vcol + 64],
                                       start=(q == 0), stop=(q == nq - 1))
                    return ins
                P.pe(pv, reads=[U.ptn] + ([U.ptn + "r"] if rem else []) + U.vreads, writes=[U.obn + "h%d" % h])

            def finalize_gen(seq0, i, u, sa, san, sb_, sbn):
                oa, oan = oab.next()
                onb, onn = on.next()
                ms = ["%sm%d" % (san, h) for h in range(8)]
                ssn = ["%ss%d" % (san, h) for h in range(8)]
                P.dve(lambda e: e.tensor_tensor(out=sa[:, 16:24], in0=sa[:, 0:8], in1=sink8, op=ALU.add), reads=ms + ["sink8"], writes=[san + "t"])
                P.act(lambda e: e.activation(out=sa[:, 16:24], in_=sa[:, 16:24], func=AF.Exp), reads=[san + "t"], writes=[san + "t"])
                P.dve(lambda e: e.tensor_tensor(out=sa[:, 24:32], in0=sa[:, 16:24], in1=sa[:, 8:16], op=ALU.add), reads=[san + "t"] + ssn, writes=[san + "d"])
                P.dve(lambda e: e.reciprocal(out=sa[:, 32:40], in_=sa[:, 24:32]), reads=[san + "d"], writes=[san + "r"])
                P.dve(lambda e: e.tensor_tensor(out=oa[:, 0:512].rearrange("p (h d) -> p h d", h=8),
                                                in0=bankf(6).rearrange("p (h d) -> p h d", h=8),
                                                in1=sa[:, 32:40].unsqueeze(2).broadcast_to([128, 8, 64]), op=ALU.mult),
                      reads=["pOah%d" % h for h in range(8)] + [san + "r"], writes=[oan + "a"])
                sbs = ["%ss%d" % (sbn, h) for h in range(8)]
                P.dve(lambda e: e.reciprocal(out=sb_[:, 16:24], in_=sb_[:, 8:16]), reads=sbs, writes=[sbn + "r"])
                P.dve(lambda e: e.tensor_tensor(out=oa[:, 512:1024].rearrange("p (h d) -> p h d", h=8),
                                                in0=bankf(7).rearrange("p (h d) -> p h d", h=8),
                                                in1=sb_[:, 16:24].unsqueeze(2).broadcast_to([128, 8, 64]), op=ALU.mult),
                      reads=["pObh%d" % h for h in range(8)] + [sbn + "r"], writes=[oan + "b"])
                s3, s3n = sn.next()
                P.dve(lambda e: e.scalar_tensor_tensor(out=onb[:, 0:512], in0=oa[:, 0:512], scalar=1.0, in1=oa[:, 0:512], op0=ALU.mult, op1=ALU.mult,
                                                       accum_out=s3[:, 0:1]), reads=[oan + "a"], writes=[onn + "0", s3n + "a"])
                P.dve(lambda e: e.scalar_tensor_tensor(out=onb[:, 512:1024], in0=oa[:, 512:1024], scalar=1.0, in1=oa[:, 512:1024], op0=ALU.mult, op1=ALU.mult,
                                                       accum_out=s3[:, 1:2]), reads=[oan + "b"], writes=[onn + "1", s3n + "b"])
                yield
                P.act(lambda e: e.activation(out=s3[:, 0:2], in_=s3[:, 0:2], func=AF.Ln, scale=1.0 / 512, bias=epst),
                      reads=[s3n + "a", s3n + "b", "epst"], writes=[s3n + "p"])
                P.act(lambda e: e.activation(out=s3[:, 2:4], in_=s3[:, 0:2], func=AF.Exp, scale=-0.5), reads=[s3n + "p"], writes=[s3n + "r"])
                for q in range(2):
                    P.dve(lambda e, q=q: e.scalar_tensor_tensor(out=onb[:, q * 512:(q + 1) * 512], in0=oa[:, q * 512:(q + 1) * 512],
                                                                scalar=s3[:, 2 + q:3 + q], in1=gnab[:, q * 512:(q + 1) * 512],
                                                                op0=ALU.mult, op1=ALU.mult),
                          reads=[oan + ("a" if q == 0 else "b"), s3n + "r", "gnab"], writes=[onn + "%d" % q])
                yield
                oT, oTn = onT.next()
                bi, bn = wb.next()

                def tr(e):
                    for k in range(8):
                        ins = e.transpose(out=bankb(bi)[:, k * 128:(k + 1) * 128], in_=onb[:, k * 128:(k + 1) * 128], identity=ident)
                    return ins
                P.pe(tr, reads=[onn + "0", onn + "1", "ident"], writes=[bn])
                P.dve(lambda e: e.tensor_copy(out=oT, in_=bankb(bi).rearrange("p (k n) -> p k n", k=8)), reads=[bn], writes=[oTn])
                r0 = seq0 + i * T2 + u * 128
                xr, xrn = xres.next()
                ta, tan = tt.next()
                s4, s4n = sn.next()
                P.dma(lambda e: e.dma_start(out=xr, in_=xs1_d[r0:r0 + 128, :]), writes=[xrn])
                yield

                for n in range(2):
                    def mmo(e, n=n):
                        for k in range(8):
                            ins = e.matmul(bankf(0), lhsT=oT[:, k, :], rhs=wout[:, k, n * 512:(n + 1) * 512], start=(k == 0), stop=(k == 7))
                        return ins
                    P.pe(mmo, reads=[oTn], writes=["pw0"])
                    P.act(lambda e, n=n: e.copy(out=ta[:, n * 512:(n + 1) * 512], in_=bankf(0)), reads=["pw0"], writes=[tan + "h%d" % n])
                    yield
                P.dve(lambda e: e.scalar_tensor_tensor(out=onb, in0=ta, scalar=1.0, in1=ta, op0=ALU.mult, op1=ALU.mult, accum_out=s4[:, 0:1]),
                      reads=[tan + "h0", tan + "h1"], writes=[onn + "0", onn + "1", s4n])
                P.act(lambda e: e.activation(out=s4[:, 0:1], in_=s4[:, 0:1], func=AF.Ln, scale=1.0 / D, bias=epst), reads=[s4n, "epst"], writes=[s4n])
                P.act(lambda e: e.activation(out=s4[:, 1:2], in_=s4[:, 0:1], func=AF.Exp, scale=-0.5), reads=[s4n], writes=[s4n])
                P.dve(lambda e: e.scalar_tensor_tensor(out=ta, in0=ta, scalar=s4[:, 1:2], in1=gpost, op0=ALU.mult, op1=ALU.mult),
                      reads=[tan + "h0", tan + "h1", s4n, "gpost"], writes=[tan + "h0", tan + "h1"])
                yield
                P.pool(lambda e: e.tensor_tensor(out=xr, in0=xr, in1=ta, op=ALU.add), reads=[xrn, tan + "h0", tan + "h1"], writes=[xrn])
                P.dma(lambda e: e.dma_start(out=xs2_d[r0:r0 + 128, :], in_=xr), reads=[xrn], writes=["o2_%d" % r0])

            pending = []

            def pump(filler):
                if filler is not None:
                    next(filler, None)
                for gg in list(pending):
                    try:
                        next(gg)
                    except StopIteration:
                        pending.remove(gg)

            def stage_b(seq0, S, i, g, filler):
                order = [(u, h) for u in range(2) for h in range(8)]
                stats = {}
                for u in range(2):
                    stats[u] = (sta.next(), stb.next())

                def qk_pair(u, h):
                    (sa, san), (sb_, sbn) = stats[u]
                    return (unit_qk("a", S, i, g, u, h, sa, san), unit_qk("b", S, i, g, u, h, sb_, sbn))
                cur = qk_pair(*order[0])
                for p, (u, h) in enumerate(order):
                    ua, ub = cur
                    unit_trp(ua)
                    unit_trp(ub)
                    if p + 1 < len(order):
                        cur = qk_pair(*order[p + 1])
                    pump(filler)
                    unit_pv(ua)
                    unit_pv(ub)
                    if h == 7:
                        (sa, san), (sb_, sbn) = stats[u]
                        gg = finalize_gen(seq0, i, u, sa, san, sb_, sbn)
                        next(gg)
                        pending.append(gg)

            tiles = []
            g = 0
            for (seq0, S) in ((0, SEQ_P), (SEQ_P, SEQ_S)):
                for i in range(S // T2):
                    tiles.append((seq0, S, i, g))
                    g += 1
            for g in range(2):
                for _ in stage_a_gen(*tiles[g]):
                    pass
            for g in range(len(tiles)):
                filler = stage_a_gen(*tiles[g + 2]) if g + 2 < len(tiles) else None
                seq0, S, i, _ = tiles[g]
                stage_b(seq0, S, i, g, filler)
                if filler is not None:
                    for _ in filler:
                        pass
            while pending:
                pump(None)
            P.barrier()

        if 1 in phases:
            ffn_phase(x_d, xs1_d, w1gu_d, w1d_d, 0, 0, False)
        if 2 in phases:
            mixer_phase()
        if 3 in phases:
            ffn_phase(xs2_d, y_d, w2gu_d, w2d_d, 2, 2, True)
        P.emit()
    return nc


def _win_cols():
    def hc(base, h):
        return list(base + h * 64 + np.arange(64))

    def hs(base, h):
        return list(base + h * 64 + np.concatenate([np.arange(32, 64), np.arange(0, 32)]))
    cols = []
    for c in range(4):
        cols += hc(0, c) + hc(0, 4 + c)
    for c in range(4):
        cols += hs(0, c) + hs(0, 4 + c)
    cols += hc(512, 0) + hc(512, 1)
    cols += hs(512, 0) + hs(512, 1)
    for c in range(4):
        cols += hc(768, 2 * c) + hc(768, 2 * c + 1)
    for c in range(4):
        cols += hc(1280, 2 * c) + hc(1280, 2 * c + 1)
    cols += list(range(640, 768))
    cols += list(range(1792, 2304))
    return np.asarray(cols, dtype=np.int64)


def _consts():
    inv = (10000.0 ** (-np.arange(0, 64, 2, dtype=np.float32) / np.float32(64))).astype(np.float32)
    ang = (np.arange(SEQ_S, dtype=np.float32)[:, None] * inv[None, :]).astype(np.float32)
    cos = np.cos(ang).astype(np.float32).T
    sin = np.sin(ang).astype(np.float32).T
    c64 = np.concatenate([cos, cos], 0)
    s64 = np.concatenate([-sin, sin], 0)
    c128 = np.concatenate([c64, c64], 0)
    s128 = np.concatenate([s64, s64], 0)
    rope = np.stack([c128 * np.float32(0.125), s128 * np.float32(0.125), c128, s128], 0).astype(np.float32)
    i = np.arange(128)[:, None]
    j = np.arange(384)[None, :]
    maska = np.where((j >= i) & (j <= i + 256), 0.0, NEGM).astype(np.float32)
    return np.ascontiguousarray(rope), maska


def _bias_table(rpb):
    p = np.arange(128)
    half = p // 64
    c = p % 64
    c0 = np.clip(c - 8, 0, 48)
    jj = np.arange(16)
    kc = np.arange(64)
    ri = jj[None, :] - half[:, None]
    rvalid = (ri >= 0) & (ri <= 14)
    cvalid = (kc[None, :] >= c0[:, None]) & (kc[None, :] < c0[:, None] + 16)
    cidx = np.clip(kc[None, :] - c[:, None] + 15, 0, 30)
    ric = np.clip(ri, 0, 14)
    g = rpb[:, ric[:, :, None], cidx[:, None, :]]
    valid = rvalid[:, :, None] & cvalid[:, None, :]
    tb = np.where(valid[None], g, np.float32(NEGM)).astype(np.float32)
    return np.ascontiguousarray(tb.transpose(1, 0, 2, 3).reshape(128, 8 * 16 * 64))


_CACHE = {}


def kernel(x_prompt, x_sample, ffn1_pre, ffn1_w_gu, ffn1_w_down, ffn1_post, mix_pre, w_in, sink_a, rpb_b,
           out_norm_a, out_norm_b, w_out, mix_post, ffn2_pre, ffn2_w_gu, ffn2_w_down, ffn2_post, final_norm):
    f = lambda a: np.ascontiguousarray(np.asarray(a, dtype=np.float32))
    x_prompt = f(x_prompt)
    x_sample = f(x_sample)
    if "nc" not in _CACHE:
        _CACHE["nc"] = build_program()
        _CACHE["consts"] = _consts()
    nc = _CACHE["nc"]
    rope, maska = _CACHE["consts"]

    def pk(g):
        return f(g).reshape(8, 128).T

    gpre = np.ascontiguousarray(np.concatenate([pk(ffn1_pre[0]), pk(mix_pre[0]), pk(ffn2_pre[0])], axis=1))
    gpost = np.ascontiguousarray(np.stack([np.broadcast_to(f(g[0])[None, :], (128, D)) for g in (ffn1_post, mix_post, ffn2_post, final_norm)], 0))
    gnab = np.ascontiguousarray(np.broadcast_to(np.concatenate([f(out_norm_a[0]), f(out_norm_b[0])])[None, :], (128, D)))
    sink = np.ascontiguousarray(np.broadcast_to(f(sink_a[0])[None, :], (128, 8)))
    tb = _bias_table(f(rpb_b[0]))
    win = np.ascontiguousarray(f(w_in[0])[:, _win_cols()])
    shared = {
        "w1gu": f(ffn1_w_gu[0]), "w1d": f(ffn1_w_down[0]), "w2gu": f(ffn2_w_gu[0]), "w2d": f(ffn2_w_down[0]),
        "win": win, "wout": f(w_out[0]), "gpre": gpre, "gpost": gpost, "gnab": gnab, "sink": sink, "tb": tb,
        "maska": maska, "rope": rope,
    }
    in_maps = []
    for c in range(8):
        m = dict(shared)
        m["x"] = np.ascontiguousarray(np.concatenate([x_prompt[c], x_sample[c]], axis=0))
        in_maps.append(m)
    res = run_bass_kernel_spmd(nc, in_maps, core_ids=list(range(8)))
    yp = np.stack([res.results[c]["y"][:SEQ_P] for c in range(8)], 0).astype(np.float32)
    ys = np.stack([res.results[c]["y"][SEQ_P:] for c in range(8)], 0).astype(np.float32)
    return (yp, ys)
```

```python
import contextlib
import numpy as np
import concourse.bass as bass
import concourse.mybir as mybir
from concourse.bass_utils import run_bass_kernel_spmd

F32 = mybir.dt.float32
BF16 = mybir.dt.bfloat16
AF = mybir.ActivationFunctionType
ALU = mybir.AluOpType
AX = mybir.AxisListType

D = 1024
DFF = 2816
NJ = DFF // 128
SEQ_P = 2048
SEQ_S = 8192
NTOK = SEQ_P + SEQ_S
EPS = 1e-6
NEGM = -30000.0
WIN_COLS = 2944
RING = 1024
T2 = 256
SW = 1408


class Buf:
    __slots__ = ("writer", "readers")

    def __init__(self):
        self.writer = None
        self.readers = []


class Op:
    __slots__ = ("eng", "idx", "fn", "waits", "needs_inc", "is_dma", "dsem", "dval", "incval")

    def __init__(self, eng, idx, fn, is_dma=False):
        self.eng = eng
        self.idx = idx
        self.fn = fn
        self.waits = []
        self.needs_inc = False
        self.is_dma = is_dma
        self.dsem = None
        self.dval = 0
        self.incval = 0


class Prog:
    ENGS = ("pe", "act", "dve", "pool", "sp")

    def __init__(self, nc, n_dma_sems=16):
        self.nc = nc
        self.streams = {e: [] for e in self.ENGS}
        self.seen = {e: {} for e in self.ENGS}
        self.n_dma_sems = n_dma_sems
        self.dma_rr = 0
        self.dma_last = [None] * n_dma_sems
        self.dma_cnt = [0] * n_dma_sems
        self.bufs = {}
        self.last_real = {e: None for e in self.ENGS}

    def buf(self, name):
        b = self.bufs.get(name)
        if b is None:
            b = Buf()
            self.bufs[name] = b
        return b

    def _dep(self, op, prod):
        if prod is None:
            return
        e = op.eng
        if prod.is_dma:
            key = ("d", prod.dsem)
            val = prod.dval
        else:
            key = prod.eng
            val = prod.idx
        if self.seen[e].get(key, -1) >= val:
            return
        self.seen[e][key] = val
        prod.needs_inc = True
        op.waits.append(prod)

    def op(self, eng, fn, reads=(), writes=(), is_dma=False):
        st = self.streams[eng]
        o = Op(eng, len(st), fn, is_dma)
        if is_dma:
            s = self.dma_rr
            self.dma_rr = (s + 1) % self.n_dma_sems
            prev = self.dma_last[s]
            o.dsem = s
            self.dma_cnt[s] += 1
            o.dval = self.dma_cnt[s]
            if prev is not None:
                self._dep(o, prev)
            self.dma_last[s] = o
            o.needs_inc = True
        rb = [self.buf(b) for b in reads]
        wb = [self.buf(b) for b in writes]
        for b in rb:
            self._dep(o, b.writer)
        for b in wb:
            self._dep(o, b.writer)
            for r in b.readers:
                self._dep(o, r)
        for b in rb:
            b.readers.append(o)
        for b in wb:
            b.writer = o
            b.readers = []
        st.append(o)
        if fn is not None and not is_dma:
            self.last_real[eng] = o
        return o

    def pe(self, fn, reads=(), writes=()):
        return self.op("pe", fn, reads, writes)

    def act(self, fn, reads=(), writes=()):
        return self.op("act", fn, reads, writes)

    def dve(self, fn, reads=(), writes=()):
        return self.op("dve", fn, reads, writes)

    def pool(self, fn, reads=(), writes=()):
        return self.op("pool", fn, reads, writes)

    def dma(self, fn, reads=(), writes=()):
        return self.op("sp", fn, reads, writes, is_dma=True)

    def barrier(self):
        prods = [self.last_real[e] for e in ("pe", "act", "dve", "pool")]
        prods += list(self.dma_last)
        for e in self.ENGS:
            o = Op(e, len(self.streams[e]), None)
            for p in prods:
                if p is not None:
                    self._dep(o, p)
            self.streams[e].append(o)
        self.bufs = {}

    def emit(self):
        nc = self.nc
        self.barrier()
        for e in self.ENGS:
            c = 0
            for o in self.streams[e]:
                if o.is_dma:
                    o.incval = 16 * o.dval
                elif o.needs_inc:
                    c += 1
                    o.incval = c
        with contextlib.ExitStack() as es:
            esem = {e: es.enter_context(nc.semaphore("s_" + e)) for e in self.ENGS}
            dsem = [es.enter_context(nc.semaphore("d%d" % i)) for i in range(self.n_dma_sems)]
            block = es.enter_context(nc.Block())

            def run(e, eng):
                for o in self.streams[e]:
                    for p in o.waits:
                        if p.is_dma:
                            eng.wait_ge(dsem[p.dsem], p.incval)
                        else:
                            eng.wait_ge(esem[p.eng], p.incval)
                    if o.fn is None:
                        continue
                    ins = o.fn(eng)
                    if o.is_dma:
                        ins.then_inc(dsem[o.dsem], 16)
                    elif o.needs_inc:
                        ins.then_inc(esem[e], 1)

            @block.tensor
            def _(eng):
                run("pe", eng)

            @block.scalar
            def _(eng):
                run("act", eng)

            @block.vector
            def _(eng):
                run("dve", eng)

            @block.gpsimd
            def _(eng):
                run("pool", eng)

            @block.sync
            def _(eng):
                run("sp", eng)


class Arena:
    def __init__(self, t_f32, nwords):
        self.f = t_f32
        self.b = t_f32.bitcast(BF16)
        self.off = 0
        self.cap = nwords * 4

    def alloc(self, n, dt):
        self.off = (self.off + 63) // 64 * 64
        if dt == F32:
            ap = self.f[:, self.off // 4: self.off // 4 + n]
            self.off += 4 * n
        else:
            ap = self.b[:, self.off // 2: self.off // 2 + n]
            self.off += 2 * n
        assert self.off <= self.cap, ("SBUF arena overflow", self.off, self.cap)
        return ap


class Rot:
    def __init__(self, items):
        self.items = items
        self.i = 0

    def next(self):
        it = self.items[self.i % len(self.items)]
        self.i += 1
        return it


def load_weight(P, stg, wdram, KC, N, dst, gain, eng_rr):
    for k in range(KC):
        for n0 in range(0, N, SW):
            n1 = min(N, n0 + SW)
            sap, sname = stg.next()
            P.dma(lambda e, sap=sap, k=k, n0=n0, n1=n1: e.dma_start(out=sap[:, 0:n1 - n0], in_=wdram[k * 128:(k + 1) * 128, n0:n1]),
                  writes=[sname])
            which = eng_rr[0] % 2
            eng_rr[0] += 1
            o = dst[:, k, n0:n1]
            i = sap[:, 0:n1 - n0]
            if gain is not None:
                gk = gain[:, k:k + 1]
                if which == 0:
                    P.dve(lambda e, o=o, i=i, gk=gk: e.tensor_scalar(out=o, in0=i, scalar1=gk, scalar2=None, op0=ALU.mult), reads=[sname])
                elif which == 1:
                    P.act(lambda e, o=o, i=i, gk=gk: e.activation(out=o, in_=i, func=AF.Copy, scale=gk), reads=[sname])
                else:
                    P.pool(lambda e, o=o, i=i, gk=gk: e.tensor_scalar(out=o, in0=i, scalar1=gk, scalar2=None, op0=ALU.mult), reads=[sname])
            else:
                if which == 0:
                    P.dve(lambda e, o=o, i=i: e.tensor_copy(out=o, in_=i), reads=[sname])
                elif which == 1:
                    P.act(lambda e, o=o, i=i: e.copy(out=o, in_=i), reads=[sname])
                else:
                    P.pool(lambda e, o=o, i=i: e.tensor_copy(out=o, in_=i), reads=[sname])


def build_program(debug=False, phases=(1, 2, 3)):
    nc = bass.Bass("TRN2", target_bir_lowering=False)

    def din(name, shape):
        return nc.dram_tensor(name, list(shape), F32, kind="ExternalInput").ap()

    x_d = din("x", [NTOK, D])
    w1gu_d = din("w1gu", [D, 2 * DFF])
    w1d_d = din("w1d", [DFF, D])
    w2gu_d = din("w2gu", [D, 2 * DFF])
    w2d_d = din("w2d", [DFF, D])
    win_d = din("win", [D, WIN_COLS])
    wout_d = din("wout", [D, D])
    gpre_d = din("gpre", [128, 24])
    gpost_d = din("gpost", [4, 128, D])
    gnab_d = din("gnab", [128, D])
    sink_d = din("sink", [128, 8])
    tb_d = din("tb", [128, 8 * 16 * 64])
    maska_d = din("maska", [128, 384])
    rope_d = din("rope", [4, 128, SEQ_S])
    y_d = nc.dram_tensor("y", [NTOK, D], F32, kind="ExternalOutput").ap()
    skind = "ExternalOutput" if debug else "Internal"
    xs1_d = nc.dram_tensor("xs1", [NTOK, D], F32, kind=skind).ap()
    xs2_d = nc.dram_tensor("xs2", [NTOK, D], F32, kind=skind).ap()

    with contextlib.ExitStack() as es:
        NW = 53200
        arena_t = es.enter_context(nc.sbuf_tensor("arena", [128, NW], F32))
        ps_t = es.enter_context(nc.psum_tensor("ps", [128, 4096], F32))
        psf = ps_t[:]
        psb = ps_t[:].bitcast(BF16)
        P = Prog(nc)
        eng_rr = [0]

        def bankf(i, n=512):
            return psf[:, i * 512:i * 512 + n]

        def bankb(i, n=1024):
            return psb[:, i * 1024:i * 1024 + n]

        def make_ident(A):
            identf = A.alloc(128, F32)
            ident = A.alloc(128, BF16)
            P.pool(lambda e: e.memset(identf, 0.0), writes=["identf"])
            P.pool(lambda e: e.affine_select(out=identf, in_=identf, pattern=[[-1, 128]], compare_op=ALU.not_equal,
                                             fill=1.0, base=0, channel_multiplier=1), reads=["identf"], writes=["identf"])
            P.dve(lambda e: e.tensor_copy(out=ident, in_=identf), reads=["identf"], writes=["ident"])
            return identf, ident

        def ffn_phase(src_d, dst_d, wgu_d, wd_d, gidx, pidx, final):
            A = Arena(arena_t[:], NW)
            wgu = A.alloc(8 * 2 * DFF, BF16).rearrange("p (k n) -> p k n", k=8)
            wd = A.alloc(NJ * D, BF16).rearrange("p (k n) -> p k n", k=NJ)
            gpre = A.alloc(8, F32)
            gpost = A.alloc(D, F32)
            gfin = A.alloc(D, F32) if final else None
            epst = A.alloc(1, F32)
            identf, ident = make_ident(A)
            P.pool(lambda e: e.memset(epst, EPS), writes=["epst"])
            P.dma(lambda e: e.dma_start(out=gpre, in_=gpre_d[:, gidx * 8:(gidx + 1) * 8]), writes=["gpre"])
            P.dma(lambda e: e.dma_start(out=gpost, in_=gpost_d[pidx]), writes=["gpost"])
            P.dve(lambda e: e.tensor_scalar(out=gpost, in0=gpost, scalar1=0.5, scalar2=None, op0=ALU.mult), reads=["gpost"], writes=["gpost"])
            if final:
                P.dma(lambda e: e.dma_start(out=gfin, in_=gpost_d[3]), writes=["gfin"])
            mark = A.off
            stg = Rot([(A.alloc(SW, F32), "stg%d" % i) for i in range(3)])
            P.barrier()
            load_weight(P, stg, wgu_d, 8, 2 * DFF, wgu, gpre, eng_rr)
            load_weight(P, stg, wd_d, NJ, D, wd, None, eng_rr)
            P.barrier()
            A.off = mark
            xt = Rot([(A.alloc(D, F32), "xt%d" % i) for i in range(2)])
            xn = [(A.alloc(D, BF16), "xn%d" % i) for i in range(4)]
            xnT = A.alloc(8 * 512, BF16).rearrange("p (k n) -> p k n", k=8)
            h2T = A.alloc(NJ * 512, BF16).rearrange("p (k n) -> p k n", k=NJ)
            sg = Rot([(A.alloc(512, BF16), "sg%d" % i) for i in range(2)])
            xres = Rot([(A.alloc(D, F32), "xres%d" % i) for i in range(2)])
            tt = Rot([(A.alloc(D, F32), "tt%d" % i) for i in range(2)])
            ss4 = Rot([(A.alloc(4, F32), "ss4_%d" % i) for i in range(2)])
            rs4 = Rot([(A.alloc(4, F32), "rs4_%d" % i) for i in range(2)])
            sp1 = Rot([(A.alloc(2, F32), "sp1_%d" % i) for i in range(3)])
            sf1 = Rot([(A.alloc(2, F32), "sf1_%d" % i) for i in range(3)])
            gub = Rot([(i, "pb%d" % i) for i in range(4)])
            dbank = Rot([(psf[:, 2048:3072], "pd0"), (psf[:, 3072:4096], "pd1")])
            ntile = NTOK // 512

            cur = {}

            def prep_load(t, pair):
                if pair == 0:
                    cur["s4"] = ss4.next()
                    cur["r4"] = rs4.next()
                    cur["subs"] = {}
                s4, s4n = cur["s4"]
                for s in (2 * pair, 2 * pair + 1):
                    xa, xname = xt.next()
                    xo, xon = xn[s]
                    r0 = t * 512 + s * 128
                    P.dma(lambda e, xa=xa, r0=r0: e.dma_start(out=xa, in_=src_d[r0:r0 + 128, :]), writes=[xname])
                    P.dve(lambda e, xa=xa, s=s, s4=s4, xo=xo: e.scalar_tensor_tensor(out=xo, in0=xa, scalar=1.0, in1=xa, op0=ALU.mult, op1=ALU.mult,
                                                                                       accum_out=s4[:, s:s + 1]),
                          reads=[xname], writes=[xon, s4n + "_%d" % s])
                    cur["subs"][s] = (xa, xname)

            def prep_sqrt(t, pair):
                s4, s4n = cur["s4"]
                r4, r4n = cur["r4"]
                lo = 2 * pair
                P.act(lambda e, s4=s4, lo=lo: e.activation(out=s4[:, lo:lo + 2], in_=s4[:, lo:lo + 2], func=AF.Sqrt, scale=1.0 / D, bias=epst),
                      reads=[s4n + "_%d" % lo, s4n + "_%d" % (lo + 1), "epst"], writes=[s4n + "p%d" % lo])
                P.dve(lambda e, s4=s4, r4=r4, lo=lo: e.reciprocal(out=r4[:, lo:lo + 2], in_=s4[:, lo:lo + 2]),
                      reads=[s4n + "p%d" % lo], writes=[r4n + "p%d" % lo])

            def prep_scale(t, pair):
                r4, r4n = cur["r4"]
                lo = 2 * pair
                for q in (lo, lo + 1):
                    xa2, xname2 = cur["subs"][q]
                    xo, xon = xn[q]
                    P.act(lambda e, xo=xo, xa2=xa2, r4=r4, q=q: e.activation(out=xo, in_=xa2, func=AF.Copy, scale=r4[:, q:q + 1]),
                          reads=[xname2, r4n + "p%d" % lo], writes=[xon])

            def prep_T(s):
                xo, xon = xn[s]
                bi, bn = gub.next()

                def tr(e):
                    for k in range(8):
                        i = e.transpose(out=bankb(bi)[:, k * 128:(k + 1) * 128], in_=xo[:, k * 128:(k + 1) * 128], identity=ident)
                    return i
                P.pe(tr, reads=[xon, "ident"], writes=[bn])
                P.dve(lambda e, s=s: e.tensor_copy(out=xnT[:, :, s * 128:(s + 1) * 128], in_=bankb(bi).rearrange("p (k n) -> p k n", k=8)),
                      reads=[bn], writes=["xnT"])

            def gu(j):
                gi, gbn = gub.next()
                gb = bankf(gi)
                sga, sgn = sg.next()

                def mmg(e):
                    for k in range(8):
                        i = e.matmul(gb, lhsT=wgu[:, k, j * 128:(j + 1) * 128], rhs=xnT[:, k, :], start=(k == 0), stop=(k == 7))
                    return i
                P.pe(mmg, reads=["xnT"], writes=[gbn])
                P.act(lambda e: e.activation(out=sga, in_=gb, func=AF.Silu), reads=[gbn], writes=[sgn])
                ui, ubn = gub.next()
                ub = bankf(ui)

                def mmu(e):
                    for k in range(8):
                        i = e.matmul(ub, lhsT=wgu[:, k, DFF + j * 128:DFF + (j + 1) * 128], rhs=xnT[:, k, :], start=(k == 0), stop=(k == 7))
                    return i
                P.pe(mmu, reads=["xnT"], writes=[ubn])
                P.dve(lambda e: e.tensor_tensor(out=h2T[:, j, :], in0=ub, in1=sga, op=ALU.mult), reads=[ubn, sgn], writes=["h2T"])

            deferred = []

            def down(t, s):
                db, dbn = dbank.next()

                def mm(e):
                    for n in range(2):
                        for j in range(NJ):
                            i = e.matmul(db[:, n * 512:(n + 1) * 512], lhsT=h2T[:, j, s * 128:(s + 1) * 128], rhs=wd[:, j, n * 512:(n + 1) * 512],
                                         start=(j == 0), stop=(j == NJ - 1))
                    return i
                P.pe(mm, reads=["h2T"], writes=[dbn])
                r0 = t * 512 + s * 128
                xr, xrn = xres.next()
                ta, tan = tt.next()
                s1, s1n = sp1.next()
                f1, f1n = sf1.next()
                P.dma(lambda e: e.dma_start(out=xr, in_=src_d[r0:r0 + 128, :]), writes=[xrn])
                P.act(lambda e: e.activation(out=ta, in_=db, func=AF.Square, accum_out=s1[:, 0:1]), reads=[dbn], writes=[tan, s1n])
                P.act(lambda e: e.activation(out=s1[:, 0:1], in_=s1[:, 0:1], func=AF.Sqrt, scale=1.0 / D, bias=epst), reads=[s1n, "epst"], writes=[s1n])
                P.dve(lambda e: e.reciprocal(out=s1[:, 1:2], in_=s1[:, 0:1]), reads=[s1n], writes=[s1n])
                P.dve(lambda e: e.scalar_tensor_tensor(out=ta, in0=db, scalar=s1[:, 1:2], in1=gpost, op0=ALU.mult, op1=ALU.mult),
                      reads=[dbn, s1n, "gpost"], writes=[tan])
                if not final:
                    P.pool(lambda e: e.tensor_tensor(out=xr, in0=xr, in1=ta, op=ALU.add), reads=[xrn, tan], writes=[xrn])
                    P.dma(lambda e: e.dma_start(out=dst_d[r0:r0 + 128, :], in_=xr), reads=[xrn], writes=["out%d" % r0])
                else:
                    prev = list(deferred)
                    del deferred[:]
                    for fn in prev:
                        fn()
                    P.pool(lambda e: e.tensor_tensor(out=xr, in0=xr, in1=ta, op=ALU.add), reads=[xrn, tan], writes=[xrn])

                    def tail():
                        P.dve(lambda e: e.scalar_tensor_tensor(out=ta, in0=xr, scalar=1.0, in1=xr, op0=ALU.mult, op1=ALU.mult, accum_out=f1[:, 0:1]),
                              reads=[xrn], writes=[tan, f1n])
                        P.act(lambda e: e.activation(out=f1[:, 0:1], in_=f1[:, 0:1], func=AF.Sqrt, scale=1.0 / D, bias=epst), reads=[f1n, "epst"], writes=[f1n])
                        P.dve(lambda e: e.reciprocal(out=f1[:, 1:2], in_=f1[:, 0:1]), reads=[f1n], writes=[f1n])
                        P.dve(lambda e: e.scalar_tensor_tensor(out=ta, in0=xr, scalar=f1[:, 1:2], in1=gfin, op0=ALU.mult, op1=ALU.mult),
                              reads=[xrn, f1n, "gfin"], writes=[tan])
                        P.dma(lambda e: e.dma_start(out=dst_d[r0:r0 + 128, :], in_=ta), reads=[tan], writes=["out%d" % r0])
                    deferred.append(tail)

            for pair in range(2):
                prep_load(0, pair)
                prep_sqrt(0, pair)
                prep_scale(0, pair)
            for s in range(4):
                prep_T(s)
            sched = {1: (prep_load, 0), 4: (prep_sqrt, 0), 5: (prep_scale, 0), 8: (prep_load, 1), 11: (prep_sqrt, 1), 12: (prep_scale, 1)}
            for t in range(ntile):
                for j in range(NJ):
                    gu(j)
                    if j in sched and t + 1 < ntile:
                        fn, pair = sched[j]
                        fn(t + 1, pair)
                for s in range(4):
                    if t + 1 < ntile:
                        prep_T(s)
                    down(t, s)
            for fn in deferred:
                fn()
            P.barrier()

        def mixer_phase():
            A = Arena(arena_t[:], NW)
            win = A.alloc(8 * WIN_COLS, BF16).rearrange("p (k n) -> p k n", k=8)
            wout = A.alloc(8 * D, BF16).rearrange("p (k n) -> p k n", k=8)
            tb = A.alloc(8 * 16 * 64, BF16)
            maska = A.alloc(384, BF16)
            negt = A.alloc(64, BF16)
            gmix = A.alloc(8, F32)
            gpost = A.alloc(D, F32)
            gnab = A.alloc(D, F32)
            sink8 = A.alloc(8, F32)
            epst = A.alloc(1, F32)
            identf, ident = make_ident(A)
            idh0 = A.alloc(128, BF16)
            idh1 = A.alloc(128, BF16)
            P.pool(lambda e: e.memset(epst, EPS), writes=["epst"])
            P.pool(lambda e: e.memset(negt, NEGM), writes=["negt"])
            P.pool(lambda e: e.memset(idh0, 0.0), writes=["idh0"])
            P.pool(lambda e: e.memset(idh1, 0.0), writes=["idh1"])
            P.dve(lambda e: e.tensor_copy(out=idh0[0:64, :], in_=identf[0:64, :]), reads=["identf", "idh0"], writes=["idh0"])
            P.dve(lambda e: e.tensor_copy(out=idh1[64:128, :], in_=identf[64:128, :]), reads=["identf", "idh1"], writes=["idh1"])
            P.dma(lambda e: e.dma_start(out=gmix, in_=gpre_d[:, 8:16]), writes=["gmix"])
            P.dma(lambda e: e.dma_start(out=gpost, in_=gpost_d[1]), writes=["gpost"])
            P.dma(lambda e: e.dma_start(out=gnab, in_=gnab_d), writes=["gnab"])
            P.dma(lambda e: e.dma_start(out=sink8, in_=sink_d), writes=["sink8"])
            mark = A.off
            stg = Rot([(A.alloc(SW, F32), "stg%d" % i) for i in range(3)])
            P.barrier()
            load_weight(P, stg, win_d, 8, WIN_COLS, win, gmix, eng_rr)
            load_weight(P, stg, wout_d, 8, D, wout, None, eng_rr)
            for n0 in range(0, 8 * 16 * 64, SW):
                n1 = min(8 * 16 * 64, n0 + SW)
                sap, sname = stg.next()
                P.dma(lambda e, sap=sap, n0=n0, n1=n1: e.dma_start(out=sap[:, 0:n1 - n0], in_=tb_d[:, n0:n1]), writes=[sname])
                P.dve(lambda e, sap=sap, n0=n0, n1=n1: e.tensor_copy(out=tb[:, n0:n1], in_=sap[:, 0:n1 - n0]), reads=[sname])
            sap, sname = stg.next()
            P.dma(lambda e, sap=sap: e.dma_start(out=sap[:, 0:384], in_=maska_d), writes=[sname])
            P.dve(lambda e, sap=sap: e.tensor_copy(out=maska, in_=sap[:, 0:384]), reads=[sname])
            P.barrier()
            A.off = mark
            tbv = tb.rearrange("p (h j c) -> p h j c", h=8, j=16)
            NBLK = RING // 128
            KaT = A.alloc(RING, BF16)
            Va = A.alloc(NBLK * 128, BF16).rearrange("p (b n) -> p b n", b=NBLK)
            KbT = A.alloc(4 * RING, BF16).rearrange("p (c n) -> p c n", c=4)
            Vb = A.alloc(NBLK * 512, BF16).rearrange("p (b n) -> p b n", b=NBLK)
            QaT = [A.alloc(4 * T2, BF16).rearrange("p (c n) -> p c n", c=4) for _ in range(3)]
            QbT = [A.alloc(4 * T2, BF16).rearrange("p (c n) -> p c n", c=4) for _ in range(3)]
            xnT = A.alloc(8 * T2, BF16).rearrange("p (k n) -> p k n", k=8)
            xt = Rot([(A.alloc(D, F32), "xt%d" % i) for i in range(2)])
            xn = [(A.alloc(D, BF16), "xn%d" % i) for i in range(2)]
            ropeb = Rot([(A.alloc(4 * T2, F32).rearrange("p (f n) -> p f n", f=4), "rope%d" % i) for i in range(2)])
            t1 = Rot([(A.alloc(T2, F32), "t1_%d" % i) for i in range(2)])
            t2 = Rot([(A.alloc(T2, F32), "t2_%d" % i) for i in range(2)])
            Pa = Rot([(A.alloc(384, BF16), "Pa%d" % i) for i in range(2)])
            PTa = Rot([(A.alloc(384, BF16), "PTa%d" % i) for i in range(2)])
            Pb = Rot([(A.alloc(576, BF16), "Pb%d" % i) for i in range(2)])
            PTb = Rot([(A.alloc(640, BF16), "PTb%d" % i) for i in range(2)])
            oab = Rot([(A.alloc(D, F32), "oab%d" % i) for i in range(2)])
            on = Rot([(A.alloc(D, BF16), "on%d" % i) for i in range(2)])
            onT = Rot([(A.alloc(D, BF16).rearrange("p (k n) -> p k n", k=8), "onT%d" % i) for i in range(2)])
            xres = Rot([(A.alloc(D, F32), "xres%d" % i) for i in range(2)])
            tt = Rot([(A.alloc(D, F32), "tt%d" % i) for i in range(2)])
            st2 = Rot([(A.alloc(4, F32), "st2_%d" % i) for i in range(2)])
            sta = Rot([(A.alloc(40, F32), "sta%d" % i) for i in range(4)])
            stb = Rot([(A.alloc(24, F32), "stb%d" % i) for i in range(4)])
            sn = Rot([(A.alloc(4, F32), "sn%d" % i) for i in range(6)])
            wb = Rot([(0, "pw0")])
            abank = Rot([(1, "pA1"), (2, "pA2")])

            def ring_pieces(tok0, n):
                out = []
                off = 0
                while off < n:
                    c = (tok0 + off) % RING
                    ln = min(n - off, RING - c)
                    out.append((c, ln, off))
                    off += ln
                return out

            def stage_a_gen(seq0, S, i, g):
                tok0 = i * T2
                sl = g % 3
                s2, s2n = st2.next()
                subs = []
                for s in range(2):
                    xa, xname = xt.next()
                    r0 = seq0 + tok0 + s * 128
                    P.dma(lambda e, xa=xa, r0=r0: e.dma_start(out=xa, in_=xs1_d[r0:r0 + 128, :]), writes=[xname])
                    P.dve(lambda e, xa=xa, s=s: e.scalar_tensor_tensor(out=xn[s][0], in0=xa, scalar=1.0, in1=xa, op0=ALU.mult, op1=ALU.mult,
                                                                        accum_out=s2[:, s:s + 1]),
                          reads=[xname], writes=[xn[s][1], s2n + "_%d" % s])
                    subs.append((xa, xname))
                rp, rpn = ropeb.next()
                P.dma(lambda e: e.dma_start(out=rp, in_=rope_d[:, :, tok0:tok0 + T2].rearrange("f p n -> p f n")), writes=[rpn])
                yield
                P.act(lambda e: e.activation(out=s2[:, 0:2], in_=s2[:, 0:2], func=AF.Ln, scale=1.0 / D, bias=epst),
                      reads=[s2n + "_0", s2n + "_1", "epst"], writes=[s2n + "p"])
                P.act(lambda e: e.activation(out=s2[:, 2:4], in_=s2[:, 0:2], func=AF.Exp, scale=-0.5), reads=[s2n + "p"], writes=[s2n + "r"])
                yield
                for s in range(2):
                    xa, xname = subs[s]
                    xo, xon = xn[s]
                    P.act(lambda e, xo=xo, xa=xa, s=s: e.activation(out=xo, in_=xa, func=AF.Copy, scale=s2[:, 2 + s:3 + s]),
                          reads=[xname, s2n + "r"], writes=[xon])
                yield
                for s in range(2):
                    xo, xon = xn[s]
                    bi, bn = wb.next()

                    def tr(e, xo=xo, bi=bi):
                        for k in range(8):
                            ins = e.transpose(out=bankb(bi)[:, k * 128:(k + 1) * 128], in_=xo[:, k * 128:(k + 1) * 128], identity=ident)
                        return ins
                    P.pe(tr, reads=[xon, "ident"], writes=[bn])
                    P.dve(lambda e, s=s, bi=bi: e.tensor_copy(out=xnT[:, :, s * 128:(s + 1) * 128], in_=bankb(bi).rearrange("p (k n) -> p k n", k=8)),
                          reads=[bn], writes=["xnT"])
                    yield
                rc = tok0 % RING
                kblk = ["kv%d" % ((tok0 // 128 + s) % NBLK) for s in range(2)]

                def wmm(e, bi, col0, half):
                    for k in range(8):
                        ins = e.matmul(bankf(bi)[:, half * T2:(half + 1) * T2], lhsT=win[:, k, col0:col0 + 128], rhs=xnT[:, k, :],
                                       start=(k == 0), stop=(k == 7))
                    return ins
                for c in range(5):
                    bi, bn = wb.next()
                    pc, sc = (c, 4 + c) if c < 4 else (8, 9)

                    def mm(e, bi=bi, pc=pc, sc=sc):
                        wmm(e, bi, pc * 128, 0)
                        return wmm(e, bi, sc * 128, 1)
                    P.pe(mm, reads=["xnT"], writes=[bn])
                    ta, tan = t1.next()
                    tb2, tbn = t2.next()
                    f0 = 0 if c < 4 else 2
                    P.dve(lambda e, bi=bi, ta=ta, f0=f0: e.tensor_tensor(out=ta, in0=bankf(bi)[:, 0:T2], in1=rp[:, f0, :], op=ALU.mult),
                          reads=[bn, rpn], writes=[tan])
                    P.dve(lambda e, bi=bi, tb2=tb2, f0=f0: e.tensor_tensor(out=tb2, in0=bankf(bi)[:, T2:2 * T2], in1=rp[:, f0 + 1, :], op=ALU.mult),
                          reads=[bn, rpn], writes=[tbn])
                    if c < 4:
                        dst = QaT[sl][:, c, :]
                        wn = ["QaT%d" % sl]
                    else:
                        dst = KaT[:, rc:rc + T2]
                        wn = ["ka" + b for b in kblk]
                    P.pool(lambda e, dst=dst, ta=ta, tb2=tb2: e.tensor_tensor(out=dst, in0=ta, in1=tb2, op=ALU.add), reads=[tan, tbn], writes=wn)
                    yield
                for c in range(4):
                    bi, bn = wb.next()

                    def mm(e, bi=bi, c=c):
                        wmm(e, bi, (10 + c) * 128, 0)
                        return wmm(e, bi, (14 + c) * 128, 1)
                    P.pe(mm, reads=["xnT"], writes=[bn])
                    P.act(lambda e, bi=bi, c=c: e.activation(out=QbT[sl][:, c, :], in_=bankf(bi)[:, 0:T2], func=AF.Copy, scale=0.125),
                          reads=[bn], writes=["QbT%d" % sl])
                    P.act(lambda e, bi=bi, c=c: e.copy(out=KbT[:, c, rc:rc + T2], in_=bankf(bi)[:, T2:2 * T2]),
                          reads=[bn], writes=["kb" + b for b in kblk])
                    yield
                for s in range(2):
                    blk = (tok0 // 128 + s) % NBLK
                    bi, bn = wb.next()

                    def mmva(e, bi=bi, s=s):
                        for k in range(8):
                            ins = e.matmul(bankf(bi)[:, 0:128], lhsT=xnT[:, k, s * 128:(s + 1) * 128], rhs=win[:, k, 2304:2432], start=(k == 0), stop=(k == 7))
                        return ins
                    P.pe(mmva, reads=["xnT"], writes=[bn])
                    P.act(lambda e, bi=bi, blk=blk: e.copy(out=Va[:, blk, :], in_=bankf(bi)[:, 0:128]), reads=[bn], writes=["va" + kblk[s]])
                    yield
                    bi2, bn2 = wb.next()

                    def mmvb(e, bi2=bi2, s=s):
                        for k in range(8):
                            ins = e.matmul(bankf(bi2), lhsT=xnT[:, k, s * 128:(s + 1) * 128], rhs=win[:, k, 2432:2944], start=(k == 0), stop=(k == 7))
                        return ins
                    P.pe(mmvb, reads=["xnT"], writes=[bn2])
                    P.dve(lambda e, bi2=bi2, blk=blk: e.tensor_copy(out=Vb[:, blk, :], in_=bankf(bi2)), reads=[bn2], writes=["vb" + kblk[s]])
                    yield

            class Unit:
                pass

            def unit_qk(kind, S, i, g, u, h, stat, statn):
                U = Unit()
                sl = g % 3
                qtok0 = i * T2 + u * 128
                U.kind, U.h, U.stat, U.statn = kind, h, stat, statn
                if kind == "a":
                    nb = S // 128
                    b = qtok0 // 128
                    kb0 = max(b - 1, 0)
                    kb1 = min(b + 1, nb - 1)
                    nk = (kb1 - kb0 + 1) * 128
                    ktok0 = kb0 * 128
                    pb = 0 if h < 4 else 64
                    lhsT = QaT[sl][pb:pb + 64, h % 4, u * 128:(u + 1) * 128]
                    qn = "QaT%d" % sl
                    abi, abn = abank.next()
                    sbank = psf[:, abi * 512:abi * 512 + nk]
                    sname = [abn]
                    mcol0 = (kb0 - (b - 1)) * 128
                    U.vcol = (h // 4) * 64
                    U.ptv = bankb(abi)[:, 0:384]
                    U.pbuf, U.pbn = Pa.next()
                    U.ptbuf, U.ptn = PTa.next()
                    U.ob = bankf(6)
                    U.obn = "pOa"
                    U.V = Va
                    U.ptname = abn
                    nr = 0
                else:
                    rows = S // 64
                    r = qtok0 // 64
                    if r < 4:
                        kr0, nr = 0, 8
                    elif r >= rows - 4:
                        kr0, nr = rows - 8, 8
                    else:
                        kr0, nr = r - 4, 9
                    nk = nr * 64
                    ktok0 = kr0 * 64
                    pb = (h % 2) * 64
                    lhsT = QbT[sl][pb:pb + 64, h // 2, u * 128:(u + 1) * 128]
                    qn = "QbT%d" % sl
                    sbank = psf[:, 3 * 512:3 * 512 + nk]
                    sname = ["pB"]
                    j0 = kr0 - r + 7
                    U.vcol = h * 64
                    U.ptv = bankb(5)[:, 0:640]
                    U.pbuf, U.pbn = Pb.next()
                    U.ptbuf, U.ptn = PTb.next()
                    U.ob = bankf(7)
                    U.obn = "pOb"
                    U.V = Vb
                    U.ptname = "pPTb"
                U.nk = nk
                U.nfull = nk // 128
                U.rem = nk - U.nfull * 128
                U.kblks = [((ktok0 // 128 + q) % NBLK) for q in range(U.nfull + (1 if U.rem else 0))]
                kpre = "ka" if kind == "a" else "kb"
                vpre = "va" if kind == "a" else "vb"
                kreads = [kpre + "kv%d" % q for q in U.kblks]
                U.vreads = [vpre + "kv%d" % q for q in U.kblks]

                def qk(e):
                    segs = [(0, min(nk, 512))] + ([(512, nk)] if nk > 512 else [])
                    for (a0, a1) in segs:
                        if kind == "a":
                            e.matmul(sbank[:, a0:a1], lhsT=ident, rhs=maska[:, mcol0 + a0:mcol0 + a1], start=True, stop=False)
                        else:
                            rhs = tbv[:, h, j0:j0 + nr, :].rearrange("p j c -> p (j c)")
                            e.matmul(sbank[:, a0:a1], lhsT=ident, rhs=rhs[:, a0:a1], start=True, stop=False)
                    if kind == "b" and nr == 9:
                        e.matmul(sbank[:, 512:576], lhsT=idh0, rhs=negt, start=False, stop=False)
                        e.matmul(sbank[:, 0:64], lhsT=idh1, rhs=negt, start=False, stop=False)
                    pieces = []
                    for (c0, ln, off) in ring_pieces(ktok0, nk):
                        if off < 512 < off + ln:
                            pieces.append((c0, 512 - off, off))
                            pieces.append((c0 + 512 - off, off + ln - 512, 512))
                        else:
                            pieces.append((c0, ln, off))
                    ins = None
                    for pi, (c0, ln, off) in enumerate(pieces):
                        last_in_seg = (pi == len(pieces) - 1) or (pieces[pi + 1][2] >= 512 > off)
                        if kind == "a":
                            rhs = KaT[pb:pb + 64, c0:c0 + ln]
                        else:
                            rhs = KbT[pb:pb + 64, h // 2, c0:c0 + ln]
                        ins = e.matmul(sbank[:, off:off + ln], lhsT=lhsT, rhs=rhs, start=False, stop=last_in_seg)
                    return ins
                P.pe(qk, reads=[qn, "ident"] + kreads, writes=sname)
                P.dve(lambda e: e.tensor_reduce(out=stat[:, h:h + 1], in_=sbank, axis=AX.X, op=ALU.max, negate=True),
                      reads=sname, writes=[statn + "m%d" % h])
                P.act(lambda e: e.activation(out=U.pbuf[:, 0:nk], in_=sbank, func=AF.Exp, bias=stat[:, h:h + 1], scale=1.0, accum_out=stat[:, 8 + h:9 + h]),
                      reads=sname + [statn + "m%d" % h], writes=[U.pbn, statn + "s%d" % h])
                return U

            def unit_trp(U):
                nfull, rem, pbuf, ptv, ptbuf = U.nfull, U.rem, U.pbuf, U.ptv, U.ptbuf

                def trp(e):
                    for q in range(nfull):
                        ins = e.transpose(out=ptv[:, q * 128:(q + 1) * 128], in_=pbuf[:, q * 128:(q + 1) * 128], identity=ident)
                    if rem:
                        ins = e.transpose(out=ptv[0:rem, nfull * 128:(nfull + 1) * 128], in_=pbuf[:, nfull * 128:nfull * 128 + rem], identity=ident)
                    return ins
                P.pe(trp, reads=[U.pbn, "ident"], writes=[U.ptname])
                if U.kind == "a":
                    P.act(lambda e: e.activation(out=ptbuf[:, 0:nfull * 128], in_=ptv[:, 0:nfull * 128], func=AF.Copy), reads=[U.ptname], writes=[U.ptn])
                else:
                    P.dve(lambda e: e.tensor_copy(out=ptbuf[:, 0:nfull * 128], in_=ptv[:, 0:nfull * 128]), reads=[U.ptname], writes=[U.ptn])
                if rem:
                    P.dve(lambda e: e.tensor_copy(out=ptbuf[0:rem, nfull * 128:(nfull + 1) * 128], in_=ptv[0:rem, nfull * 128:(nfull + 1) * 128]),
                          reads=[U.ptname], writes=[U.ptn + "r"])

            def unit_pv(U):
                nfull, rem, ptbuf, h = U.nfull, U.rem, U.ptbuf, U.h

                def pv(e):
                    nq = nfull + (1 if rem else 0)
                    for q in range(nq):
                        kp = 128 if q < nfull else rem
                        ins = e.matmul(U.ob[:, h * 64:(h + 1) * 64], lhsT=ptbuf[0:kp, q * 128:(q + 1) * 128], rhs=U.V[0:kp, U.kblks[q], U.vcol:U.vcol + 64],
                                       start=(q == 0), stop=(q == nq - 1))
                    return ins
                P.pe(pv, reads=[U.ptn] + ([U.ptn + "r"] if rem else []) + U.vreads, writes=[U.obn + "h%d" % h])

            def finalize_gen(seq0, i, u, sa, san, sb_, sbn):
                oa, oan = oab.next()
                onb, onn = on.next()
                ms = ["%sm%d" % (san, h) for h in range(8)]
                ssn = ["%ss%d" % (san, h) for h in range(8)]
                P.dve(lambda e: e.tensor_tensor(out=sa[:, 16:24], in0=sa[:, 0:8], in1=sink8, op=ALU.add), reads=ms + ["sink8"], writes=[san + "t"])
                P.act(lambda e: e.activation(out=sa[:, 16:24], in_=sa[:, 16:24], func=AF.Exp), reads=[san + "t"], writes=[san + "t"])
                P.dve(lambda e: e.tensor_tensor(out=sa[:, 24:32], in0=sa[:, 16:24], in1=sa[:, 8:16], op=ALU.add), reads=[san + "t"] + ssn, writes=[san + "d"])
                P.dve(lambda e: e.reciprocal(out=sa[:, 32:40], in_=sa[:, 24:32]), reads=[san + "d"], writes=[san + "r"])
                P.dve(lambda e: e.tensor_tensor(out=oa[:, 0:512].rearrange("p (h d) -> p h d", h=8),
                                                in0=bankf(6).rearrange("p (h d) -> p h d", h=8),
                                                in1=sa[:, 32:40].unsqueeze(2).broadcast_to([128, 8, 64]), op=ALU.mult),
                      reads=["pOah%d" % h for h in range(8)] + [san + "r"], writes=[oan + "a"])
                sbs = ["%ss%d" % (sbn, h) for h in range(8)]
                P.dve(lambda e: e.reciprocal(out=sb_[:, 16:24], in_=sb_[:, 8:16]), reads=sbs, writes=[sbn + "r"])
                P.dve(lambda e: e.tensor_tensor(out=oa[:, 512:1024].rearrange("p (h d) -> p h d", h=8),
                                                in0=bankf(7).rearrange("p (h d) -> p h d", h=8),
                                                in1=sb_[:, 16:24].unsqueeze(2).broadcast_to([128, 8, 64]), op=ALU.mult),
                      reads=["pObh%d" % h for h in range(8)] + [sbn + "r"], writes=[oan + "b"])
                s3, s3n = sn.next()
                P.dve(lambda e: e.scalar_tensor_tensor(out=onb[:, 0:512], in0=oa[:, 0:512], scalar=1.0, in1=oa[:, 0:512], op0=ALU.mult, op1=ALU.mult,
                                                       accum_out=s3[:, 0:1]), reads=[oan + "a"], writes=[onn + "0", s3n + "a"])
                P.dve(lambda e: e.scalar_tensor_tensor(out=onb[:, 512:1024], in0=oa[:, 512:1024], scalar=1.0, in1=oa[:, 512:1024], op0=ALU.mult, op1=ALU.mult,
                                                       accum_out=s3[:, 1:2]), reads=[oan + "b"], writes=[onn + "1", s3n + "b"])
                yield
                P.act(lambda e: e.activation(out=s3[:, 0:2], in_=s3[:, 0:2], func=AF.Ln, scale=1.0 / 512, bias=epst),
                      reads=[s3n + "a", s3n + "b", "epst"], writes=[s3n + "p"])
                P.act(lambda e: e.activation(out=s3[:, 2:4], in_=s3[:, 0:2], func=AF.Exp, scale=-0.5), reads=[s3n + "p"], writes=[s3n + "r"])
                for q in range(2):
                    P.dve(lambda e, q=q: e.scalar_tensor_tensor(out=onb[:, q * 512:(q + 1) * 512], in0=oa[:, q * 512:(q + 1) * 512],
                                                                scalar=s3[:, 2 + q:3 + q], in1=gnab[:, q * 512:(q + 1) * 512],
                                                                op0=ALU.mult, op1=ALU.mult),
                          reads=[oan + ("a" if q == 0 else "b"), s3n + "r", "gnab"], writes=[onn + "%d" % q])
                yield
                oT, oTn = onT.next()
                bi, bn = wb.next()

                def tr(e):
                    for k in range(8):
                        ins = e.transpose(out=bankb(bi)[:, k * 128:(k + 1) * 128], in_=onb[:, k * 128:(k + 1) * 128], identity=ident)
                    return ins
                P.pe(tr, reads=[onn + "0", onn + "1", "ident"], writes=[bn])
                P.dve(lambda e: e.tensor_copy(out=oT, in_=bankb(bi).rearrange("p (k n) -> p k n", k=8)), reads=[bn], writes=[oTn])
                r0 = seq0 + i * T2 + u * 128
                xr, xrn = xres.next()
                ta, tan = tt.next()
                s4, s4n = sn.next()
                P.dma(lambda e: e.dma_start(out=xr, in_=xs1_d[r0:r0 + 128, :]), writes=[xrn])
                yield

                for n in range(2):
                    def mmo(e, n=n):
                        for k in range(8):
                            ins = e.matmul(bankf(0), lhsT=oT[:, k, :], rhs=wout[:, k, n * 512:(n + 1) * 512], start=(k == 0), stop=(k == 7))
                        return ins
                    P.pe(mmo, reads=[oTn], writes=["pw0"])
                    P.act(lambda e, n=n: e.copy(out=ta[:, n * 512:(n + 1) * 512], in_=bankf(0)), reads=["pw0"], writes=[tan + "h%d" % n])
                    yield
                P.dve(lambda e: e.scalar_tensor_tensor(out=onb, in0=ta, scalar=1.0, in1=ta, op0=ALU.mult, op1=ALU.mult, accum_out=s4[:, 0:1]),
                      reads=[tan + "h0", tan + "h1"], writes=[onn + "0", onn + "1", s4n])
                P.act(lambda e: e.activation(out=s4[:, 0:1], in_=s4[:, 0:1], func=AF.Ln, scale=1.0 / D, bias=epst), reads=[s4n, "epst"], writes=[s4n])
                P.act(lambda e: e.activation(out=s4[:, 1:2], in_=s4[:, 0:1], func=AF.Exp, scale=-0.5), reads=[s4n], writes=[s4n])
                P.dve(lambda e: e.scalar_tensor_tensor(out=ta, in0=ta, scalar=s4[:, 1:2], in1=gpost, op0=ALU.mult, op1=ALU.mult),
                      reads=[tan + "h0", tan + "h1", s4n, "gpost"], writes=[tan + "h0", tan + "h1"])
                yield
                P.pool(lambda e: e.tensor_tensor(out=xr, in0=xr, in1=ta, op=ALU.add), reads=[xrn, tan + "h0", tan + "h1"], writes=[xrn])
                P.dma(lambda e: e.dma_start(out=xs2_d[r0:r0 + 128, :], in_=xr), reads=[xrn], writes=["o2_%d" % r0])

            pending = []

            def pump(filler):
                if filler is not None:
                    next(filler, None)
                for gg in list(pending):
                    try:
                        next(gg)
                    except StopIteration:
                        pending.remove(gg)

            def stage_b(seq0, S, i, g, filler):
                order = [(u, h) for u in range(2) for h in range(8)]
                stats = {}
                for u in range(2):
                    stats[u] = (sta.next(), stb.next())

                def qk_pair(u, h):
                    (sa, san), (sb_, sbn) = stats[u]
                    return (unit_qk("a", S, i, g, u, h, sa, san), unit_qk("b", S, i, g, u, h, sb_, sbn))
                cur = qk_pair(*order[0])
                for p, (u, h) in enumerate(order):
                    ua, ub = cur
                    unit_trp(ua)
                    unit_trp(ub)
                    if p + 1 < len(order):
                        cur = qk_pair(*order[p + 1])
                    pump(filler)
                    unit_pv(ua)
                    unit_pv(ub)
                    if h == 7:
                        (sa, san), (sb_, sbn) = stats[u]
                        gg = finalize_gen(seq0, i, u, sa, san, sb_, sbn)
                        next(gg)
                        pending.append(gg)

            tiles = []
            g = 0
            for (seq0, S) in ((0, SEQ_P), (SEQ_P, SEQ_S)):
                for i in range(S // T2):
                    tiles.append((seq0, S, i, g))
                    g += 1
            for g in range(2):
                for _ in stage_a_gen(*tiles[g]):
                    pass
            for g in range(len(tiles)):
                filler = stage_a_gen(*tiles[g + 2]) if g + 2 < len(tiles) else None
                seq0, S, i, _ = tiles[g]
                stage_b(seq0, S, i, g, filler)
                if filler is not None:
                    for _ in filler:
                        pass
            while pending:
                pump(None)
            P.barrier()

        if 1 in phases:
            ffn_phase(x_d, xs1_d, w1gu_d, w1d_d, 0, 0, False)
        if 2 in phases:
            mixer_phase()
        if 3 in phases:
            ffn_phase(xs2_d, y_d, w2gu_d, w2d_d, 2, 2, True)
        P.emit()
    return nc


def _win_cols():
    def hc(base, h):
        return list(base + h * 64 + np.arange(64))

    def hs(base, h):
        return list(base + h * 64 + np.concatenate([np.arange(32, 64), np.arange(0, 32)]))
    cols = []
    for c in range(4):
        cols += hc(0, c) + hc(0, 4 + c)
    for c in range(4):
        cols += hs(0, c) + hs(0, 4 + c)
    cols += hc(512, 0) + hc(512, 1)
    cols += hs(512, 0) + hs(512, 1)
    for c in range(4):
        cols += hc(768, 2 * c) + hc(768, 2 * c + 1)
    for c in range(4):
        cols += hc(1280, 2 * c) + hc(1280, 2 * c + 1)
    cols += list(range(640, 768))
    cols += list(range(1792, 2304))
    return np.asarray(cols, dtype=np.int64)


def _consts():
    inv = (10000.0 ** (-np.arange(0, 64, 2, dtype=np.float32) / np.float32(64))).astype(np.float32)
    ang = (np.arange(SEQ_S, dtype=np.float32)[:, None] * inv[None, :]).astype(np.float32)
    cos = np.cos(ang).astype(np.float32).T
    sin = np.sin(ang).astype(np.float32).T
    c64 = np.concatenate([cos, cos], 0)
    s64 = np.concatenate([-sin, sin], 0)
    c128 = np.concatenate([c64, c64], 0)
    s128 = np.concatenate([s64, s64], 0)
    rope = np.stack([c128 * np.float32(0.125), s128 * np.float32(0.125), c128, s128], 0).astype(np.float32)
    i = np.arange(128)[:, None]
    j = np.arange(384)[None, :]
    maska = np.where((j >= i) & (j <= i + 256), 0.0, NEGM).astype(np.float32)
    return np.ascontiguousarray(rope), maska


def _bias_table(rpb):
    p = np.arange(128)
    half = p // 64
    c = p % 64
    c0 = np.clip(c - 8, 0, 48)
    jj = np.arange(16)
    kc = np.arange(64)
    ri = jj[None, :] - half[:, None]
    rvalid = (ri >= 0) & (ri <= 14)
    cvalid = (kc[None, :] >= c0[:, None]) & (kc[None, :] < c0[:, None] + 16)
    cidx = np.clip(kc[None, :] - c[:, None] + 15, 0, 30)
    ric = np.clip(ri, 0, 14)
    g = rpb[:, ric[:, :, None], cidx[:, None, :]]
    valid = rvalid[:, :, None] & cvalid[:, None, :]
    tb = np.where(valid[None], g, np.float32(NEGM)).astype(np.float32)
    return np.ascontiguousarray(tb.transpose(1, 0, 2, 3).reshape(128, 8 * 16 * 64))


_CACHE = {}


def kernel(x_prompt, x_sample, ffn1_pre, ffn1_w_gu, ffn1_w_down, ffn1_post, mix_pre, w_in, sink_a, rpb_b,
           out_norm_a, out_norm_b, w_out, mix_post, ffn2_pre, ffn2_w_gu, ffn2_w_down, ffn2_post, final_norm):
    f = lambda a: np.ascontiguousarray(np.asarray(a, dtype=np.float32))
    x_prompt = f(x_prompt)
    x_sample = f(x_sample)
    if "nc" not in _CACHE:
        _CACHE["nc"] = build_program()
        _CACHE["consts"] = _consts()
    nc = _CACHE["nc"]
    rope, maska = _CACHE["consts"]

    def pk(g):
        return f(g).reshape(8, 128).T

    gpre = np.ascontiguousarray(np.concatenate([pk(ffn1_pre[0]), pk(mix_pre[0]), pk(ffn2_pre[0])], axis=1))
    gpost = np.ascontiguousarray(np.stack([np.broadcast_to(f(g[0])[None, :], (128, D)) for g in (ffn1_post, mix_post, ffn2_post, final_norm)], 0))
    gnab = np.ascontiguousarray(np.broadcast_to(np.concatenate([f(out_norm_a[0]), f(out_norm_b[0])])[None, :], (128, D)))
    sink = np.ascontiguousarray(np.broadcast_to(f(sink_a[0])[None, :], (128, 8)))
    tb = _bias_table(f(rpb_b[0]))
    win = np.ascontiguousarray(f(w_in[0])[:, _win_cols()])
    shared = {
        "w1gu": f(ffn1_w_gu[0]), "w1d": f(ffn1_w_down[0]), "w2gu": f(ffn2_w_gu[0]), "w2d": f(ffn2_w_down[0]),
        "win": win, "wout": f(w_out[0]), "gpre": gpre, "gpost": gpost, "gnab": gnab, "sink": sink, "tb": tb,
        "maska": maska, "rope": rope,
    }
    in_maps = []
    for c in range(8):
        m = dict(shared)
        m["x"] = np.ascontiguousarray(np.concatenate([x_prompt[c], x_sample[c]], axis=0))
        in_maps.append(m)
    res = run_bass_kernel_spmd(nc, in_maps, core_ids=list(range(8)))
    yp = np.stack([res.results[c]["y"][:SEQ_P] for c in range(8)], 0).astype(np.float32)
    ys = np.stack([res.results[c]["y"][SEQ_P:] for c in range(8)], 0).astype(np.float32)
    return (yp, ys)
```
